# Optimizing a Trainium2 kernel written in Bass

```python
import math
import jax, jax.numpy as jnp
from jax import lax
import numpy as np

D_MODEL = 1024
BATCH = 2
SEQ = 16384
DEPTH = 4

D_MIX = D_MODEL
D_POOL = D_MIX // 2
D_SGU = D_MIX // 2
POOL_WINDOWS = (2, 4, 8, 16)
N_POOL_GROUPS = len(POOL_WINDOWS)
POOL_GROUP_DIM = D_POOL // N_POOL_GROUPS
CHUNK = 128
SGU_HEADS = 4
SGU_HEAD_DIM = D_SGU // SGU_HEADS
D_IN = D_POOL + 2 * D_SGU
D_FF = 2816
CONV_WIDTH = 3
N_MOD = 6
DEEPNORM_ALPHA = (2.0 * DEPTH) ** 0.25
DEEPNORM_BETA = (8.0 * DEPTH) ** -0.25
LN_EPS = 1e-5

kernel_name = "hybrid_pool_sgu_convffn_deepnorm_adaln"


def _layernorm(x, g, b):
    xf = x.astype(jnp.float32)
    mu = jnp.mean(xf, axis=-1, keepdims=True)
    var = jnp.mean(jnp.square(xf - mu), axis=-1, keepdims=True)
    y = (xf - mu) * lax.rsqrt(var + LN_EPS)
    return (y * g.astype(jnp.float32) + b.astype(jnp.float32)).astype(x.dtype)


def _modulate(x, shift, scale):
    return x * (1.0 + scale[:, None, :]) + shift[:, None, :]


def _pool_mixer(a, pool_w, pool_scale):
    B, S, _ = a.shape
    ag = a.reshape(B, S, N_POOL_GROUPS, POOL_GROUP_DIM).astype(jnp.float32)
    cs = jnp.cumsum(ag, axis=1)
    t = jnp.arange(S)
    pooled = []
    for g, w in enumerate(POOL_WINDOWS):
        csg = cs[:, :, g]
        prev = jnp.pad(csg, ((0, 0), (w, 0), (0, 0)))[:, :S]
        cnt = jnp.minimum(t + 1, w).astype(jnp.float32)[None, :, None]
        pooled.append((csg - prev) / cnt)
    pooled = (jnp.stack(pooled, axis=2) - ag).astype(a.dtype)
    mixed = jnp.einsum('bsgc,gcd->bsgd', pooled, pool_w)
    return mixed.reshape(B, S, D_POOL) * pool_scale


def _sgu_mixer(u, v, ln_g, ln_b, sgu_w, sgu_b):
    B, S, _ = v.shape
    u = jax.nn.gelu(u)
    v = _layernorm(jax.nn.gelu(v), ln_g, ln_b)
    vc = v.reshape(B, S // CHUNK, CHUNK, SGU_HEADS, SGU_HEAD_DIM)
    mask = jnp.tril(jnp.ones((CHUNK, CHUNK), dtype=bool))
    w = jnp.where(mask[None], sgu_w, jnp.zeros((), sgu_w.dtype))
    z = jnp.einsum('hts,bcshd->bcthd', w, vc) + jnp.transpose(sgu_b)[None, None, :, :, None]
    return u * z.reshape(B, S, D_SGU)


def _causal_dwconv(h, w, b):
    S = h.shape[1]
    hp = jnp.pad(h, ((0, 0), (CONV_WIDTH - 1, 0), (0, 0)))
    y = b
    for k in range(CONV_WIDTH):
        y = y + hp[:, k:k + S] * w[k]
    return y


def setup_inputs(seed: int = 0) -> dict:
    key = jax.random.key(seed)
    ks = jax.random.split(key, 24)
    nrm = lambda k, shp: jax.random.normal(k, shp, dtype=jnp.float32)
    L, D = DEPTH, D_MODEL
    ada_b = jnp.concatenate([
        0.02 * nrm(ks[3], (L, 2 * D)),
        1.0 + 0.02 * nrm(ks[4], (L, D)),
        0.02 * nrm(ks[5], (L, 2 * D)),
        1.0 + 0.02 * nrm(ks[6], (L, D)),
    ], axis=-1)
    return {
        "x": nrm(ks[0], (BATCH, SEQ, D)),
        "c": nrm(ks[1], (BATCH, D)),
        "ada_w": 0.1 * D ** -0.5 * nrm(ks[2], (L, D, N_MOD * D)),
        "ada_b": ada_b,
        "w_in": D ** -0.5 * nrm(ks[7], (L, D, D_IN)),
        "pool_w": POOL_GROUP_DIM ** -0.5 * nrm(ks[8], (L, N_POOL_GROUPS, POOL_GROUP_DIM, POOL_GROUP_DIM)),
        "pool_scale": 1.0 + 0.02 * nrm(ks[9], (L, D_POOL)),
        "sgu_ln_g": 1.0 + 0.02 * nrm(ks[10], (L, D_SGU)),
        "sgu_ln_b": 0.02 * nrm(ks[11], (L, D_SGU)),
        "sgu_w": CHUNK ** -0.5 * nrm(ks[12], (L, SGU_HEADS, CHUNK, CHUNK)),
        "sgu_b": 1.0 + 0.02 * nrm(ks[13], (L, SGU_HEADS, CHUNK)),
        "w_out": DEEPNORM_BETA * D_MIX ** -0.5 * nrm(ks[14], (L, D_MIX, D)),
        "ln1_g": 1.0 + 0.02 * nrm(ks[15], (L, D)),
        "ln1_b": 0.02 * nrm(ks[16], (L, D)),
        "w_up": D ** -0.5 * nrm(ks[17], (L, D, 2 * D_FF)),
        "conv_w": 0.5 * nrm(ks[18], (L, CONV_WIDTH, D_FF)),
        "conv_b": 0.02 * nrm(ks[19], (L, D_FF)),
        "w_down": DEEPNORM_BETA * D_FF ** -0.5 * nrm(ks[20], (L, D_FF, D)),
        "ln2_g": 1.0 + 0.02 * nrm(ks[21], (L, D)),
        "ln2_b": 0.02 * nrm(ks[22], (L, D)),
    }


def reference(x, c, ada_w, ada_b, w_in, pool_w, pool_scale, sgu_ln_g, sgu_ln_b, sgu_w, sgu_b,
              w_out, ln1_g, ln1_b, w_up, conv_w, conv_b, w_down, ln2_g, ln2_b):
    c_act = jax.nn.silu(c)
    for l in range(DEPTH):
        mod = c_act @ ada_w[l] + ada_b[l]
        shift1, scale1, gate1, shift2, scale2, gate2 = jnp.split(mod, N_MOD, axis=-1)

        h = _modulate(x, shift1, scale1)
        proj = jnp.einsum('bsd,de->bse', h, w_in[l])
        a = proj[..., :D_POOL]
        u = proj[..., D_POOL:D_POOL + D_SGU]
        v = proj[..., D_POOL + D_SGU:]
        y_a = _pool_mixer(a, pool_w[l], pool_scale[l])
        y_b = _sgu_mixer(u, v, sgu_ln_g[l], sgu_ln_b[l], sgu_w[l], sgu_b[l])
        mix = jnp.concatenate([y_a, y_b], axis=-1)
        f = jnp.einsum('bse,ed->bsd', mix, w_out[l])
        x = _layernorm(DEEPNORM_ALPHA * x + gate1[:, None, :] * f, ln1_g[l], ln1_b[l])

        h = _modulate(x, shift2, scale2)
        up = jnp.einsum('bsd,df->bsf', h, w_up[l])
        g, val = up[..., :D_FF], up[..., D_FF:]
        g = _causal_dwconv(g, conv_w[l], conv_b[l])
        f = jnp.einsum('bsf,fd->bsd', jax.nn.gelu(g) * val, w_down[l])
        x = _layernorm(DEEPNORM_ALPHA * x + gate2[:, None, :] * f, ln2_g[l], ln2_b[l])
    return x
```

```python
import numpy as np
import concourse.bass as bass
import concourse.mybir as mybir
from concourse.bass_utils import run_bass_kernel_spmd

F32 = mybir.dt.float32
BF16 = mybir.dt.bfloat16
AF = mybir.ActivationFunctionType
ALU = mybir.AluOpType

D = 1024
DFF = 2816
NJ = 22
DEPTH = 4
ALPHA = (2.0 * DEPTH) ** 0.25
EPS = 1e-5
WINDOWS = (2, 4, 8, 16)
NSLOT = 7
NSC = 22
SAME_ENGINE_SYNC = True


class Buf:
    __slots__ = ("name", "w", "r", "dsem", "dcnt", "ssem", "scnt")

    def __init__(self, name):
        self.name = name
        self.w = None
        self.r = {}
        self.dsem = None
        self.dcnt = 0
        self.ssem = None
        self.scnt = 0


class Sync:
    def __init__(self, nc, same=True):
        self.nc = nc
        self.eng = {"pe": nc.tensor, "act": nc.scalar, "dve": nc.vector, "pool": nc.gpsimd, "sp": nc.sync}
        self.esem = {k: nc.alloc_semaphore("es_" + k) for k in self.eng}
        self.ecnt = {k: 0 for k in self.eng}
        self.waited = {k: {} for k in self.eng}
        self.same = same
        self.sems = {}

    def _wait(self, e, tok):
        sem, val = tok
        if sem is self.esem[e] and (e == "pe" or not self.same):
            return
        key = id(sem)
        if self.waited[e].get(key, 0) >= val:
            return
        self.eng[e].wait_ge(sem, val)
        self.waited[e][key] = val

    def _deps(self, e, reads, writes):
        for b in reads:
            if b.w is not None:
                self._wait(e, b.w)
        for b in writes:
            if b.w is not None:
                self._wait(e, b.w)
            for t in b.r.values():
                self._wait(e, t)

    def _mark(self, tok, reads, writes):
        k = id(tok[0])
        for b in reads:
            b.r[k] = tok
        for b in writes:
            b.w = tok
            b.r = {}

    def group(self, e, fns, reads=(), writes=()):
        self._deps(e, reads, writes)
        ins = None
        for fn in fns:
            ins = fn(self.eng[e])
        self.ecnt[e] += 1
        ins.then_inc(self.esem[e], 1)
        tok = (self.esem[e], self.ecnt[e])
        self._mark(tok, reads, writes)
        return tok

    def op(self, e, fn, reads=(), writes=()):
        return self.group(e, [fn], reads, writes)

    def _dsem_inc(self, q, owner, ins):
        if q == "pool":
            if owner.ssem is None:
                owner.ssem = self.nc.alloc_semaphore("ss_" + owner.name)
            owner.scnt += 16
            ins.then_inc(owner.ssem, 16)
            return (owner.ssem, owner.scnt)
        if owner.dsem is None:
            owner.dsem = self.nc.alloc_semaphore("ds_" + owner.name)
        owner.dcnt += 16
        ins.then_inc(owner.dsem, 16)
        return (owner.dsem, owner.dcnt)

    def dma(self, q, out, in_, reads=(), writes=(), owner=None, **kw):
        self._deps(q, reads, writes)
        if owner is None:
            owner = (list(writes) + list(reads))[0]
        ins = self.eng[q].dma_start(out=out, in_=in_, **kw)
        tok = self._dsem_inc(q, owner, ins)
        self._mark(tok, reads, writes)
        return tok

    def dma2(self, q, pairs, reads=(), writes=()):
        self._deps(q, reads, writes)
        owner = (list(writes) + list(reads))[0]
        for out, in_ in pairs:
            ins = self.eng[q].dma_start(out=out, in_=in_)
            tok = self._dsem_inc(q, owner, ins)
        self._mark(tok, reads, writes)
        return tok

    def wait_bufs(self, e, bufs):
        for b in bufs:
            if b.w is not None:
                self._wait(e, b.w)
            for t in b.r.values():
                self._wait(e, t)

    def barrier(self, bufs):
        for e in self.eng:
            for o in self.eng:
                if o != e and self.ecnt[o] > 0:
                    self._wait(e, (self.esem[o], self.ecnt[o]))
            self.wait_bufs(e, bufs)


def build(nl, first_chunks, ntiles, out_all):
    NCH = ntiles * 4
    NTOK = NCH * 128
    nc = bass.Bass("TRN2", target_bir_lowering=False)
    S = Sync(nc, SAME_ENGINE_SYNC)
    allbufs = []

    def mkbuf(name):
        b = Buf(name)
        allbufs.append(b)
        return b

    def din(name, shape):
        return nc.dram_tensor(name, list(shape), F32, kind="ExternalInput").ap()

    xin = din("xin", [NTOK, D])
    cT_d = din("cT", [128, 8])
    mcol_d = din("mcol", [128, 1])
    Pprev_d = din("Pprev", [128, 4, 128])
    Pcur_d = din("Pcur", [128, 4, 128])
    Pfirst_d = din("Pfirst", [128, 4, 128])
    maskT_d = din("maskT", [128, 128])
    ident_d = din("ident", [128, 128])
    ada_w = din("ada_w", [nl, D, 6 * D])
    vecsT_d = din("vecsT", [nl, 128, 160])
    w_in = din("w_in", [nl, D, 1536])
    pool_w = din("pool_w", [nl, 4, 128, 128])
    sgu_wT = din("sgu_wT", [nl, 4, 128, 128])
    sgu_ln_b = din("sgu_ln_b", [nl, 512])
    sgu_b = din("sgu_b", [nl, 512])
    w_out = din("w_out", [nl, D, D])
    ln1_g = din("ln1_g", [nl, D])
    ln1_b = din("ln1_b", [nl, D])
    w_up = din("w_up", [nl, D, 2 * DFF])
    w_down = din("w_down", [nl, DFF, D])
    ln2_g = din("ln2_g", [nl, D])
    ln2_b = din("ln2_b", [nl, D])
    n_out_ch = NCH if out_all else NCH - 4
    yout = nc.dram_tensor("yout", [n_out_ch * 128, D], F32, kind="ExternalOutput").ap()
    wsc = nc.dram_tensor("wsc", [nl, NSC, 128, 4096], BF16).ap()
    gsc = nc.dram_tensor("gsc", [nl, 2, D], F32).ap()
    b1sc = nc.dram_tensor("b1sc", [nl, 2 * D], BF16).ap()
    Bb1sc = [mkbuf("b1sc%d" % l) for l in range(nl)]
    Bwsc = [[mkbuf("wsc%d_%d" % (l, s)) for s in range(NSC)] for l in range(nl)]
    Bgsc = [[mkbuf("gsc%d_%d" % (l, s)) for s in range(2)] for l in range(nl)]
    Byout = mkbuf("yout")

    def sb(name, shape, dt=F32):
        t = nc.alloc_sbuf_tensor("s_" + name, list(shape), dt)
        return t, mkbuf(name)

    ident_bf, Bident = sb("ident_bf", [128, 128], BF16)
    ident_f, Bidentf = sb("ident_f", [128, 128])
    ones_f, Bonesf = sb("ones_f", [128, 128])
    ones_bf, Bonesbf = sb("ones_bf", [128, 128], BF16)
    Pprev, BPprev = sb("Pprev", [128, 4, 128], BF16)
    Pcur, BPcur = sb("Pcur", [128, 4, 128], BF16)
    Pfirst, BPfirst = sb("Pfirst", [128, 4, 128], BF16)
    mcol, Bmcol = sb("mcol", [128, 1])
    cact, Bcact = sb("cact", [128, 8], BF16)
    mhalf, Bmhalf = sb("mhalf", [128, 1])
    vt = [sb("vt%d" % l, [128, 160]) for l in range(nl)]
    bT2 = [sb("bT2_%d" % l, [128, 8]) for l in range(nl)]
    modT = [sb("modT%d" % l, [128, 48]) for l in range(nl)]
    gT = [sb("gT%d" % l, [128, 16]) for l in range(nl)]
    rall = [sb("rall%d" % l, [128, 48]) for l in range(nl)]
    cbp = [sb("cbp%d" % l, [128, NJ]) for l in range(nl)]
    Cc = [sb("Cc%d" % l, [128, 4, 128]) for l in range(nl)]
    rvrow = [sb("rvrow%d" % l, [1, 1024], BF16) for l in range(nl)]
    poolw = [sb("poolw%d" % l, [128, 4, 128], BF16) for l in range(nl)]
    WsT = [sb("WsT%d" % l, [128, 4, 128], BF16) for l in range(nl)]
    aprev = [sb("aprev%d" % l, [128, 512], BF16) for l in range(nl)]
    gcarry = [sb("gcarry%d" % l, [128, NJ, 2]) for l in range(nl)]

    ring = [sb("ring%d" % i, [128, 8, 512], BF16) for i in range(NSLOT)]
    ring_cnt = [0]

    def ring_next():
        i = ring_cnt[0] % NSLOT
        ring_cnt[0] += 1
        return ring[i]

    def ring_load(l, s, nk=8):
        t, B = ring_next()
        S.dma("sp", t[:, 0:nk, :], wsc[l, s, :, 0:nk * 512].rearrange("p (k n) -> p k n", n=512),
              reads=[Bwsc[l][s]], writes=[B])
        return t, B

    pbank = [(nc.alloc_psum_tensor("pb%d" % i, [128, 512], F32), mkbuf("pb%d" % i)) for i in range(6)]
    ptp = [(nc.alloc_psum_tensor("tp%d" % i, [128, 8, 128], BF16), mkbuf("tp%d" % i)) for i in range(2)]
    pcnt = [0, 0]

    def next_bank():
        i = pcnt[0] % 6
        pcnt[0] += 1
        return pbank[i]

    def next_tp():
        i = pcnt[1] % 2
        pcnt[1] += 1
        return ptp[i]

    y, _ = sb("y", [128, 4, D])
    By = [mkbuf("y%d" % c) for c in range(4)]
    yb, _ = sb("yb", [128, 4, D], BF16)
    Byb = [mkbuf("yb%d" % c) for c in range(4)]
    hT, _ = sb("hT", [128, 8, 512], BF16)
    BhT = [mkbuf("hT%d" % i) for i in range(4)]
    hid, _ = sb("hid", [128, NJ, 512], BF16)
    Bhid = [mkbuf("hid%d" % j) for j in range(NJ)]
    lnt = [[sb("lnt%d%d" % (s, i), [128, D]) for i in range(2)] for s in range(2)]
    acur, _ = sb("acur", [128, 4, 512], BF16)
    Bacur = [mkbuf("acur%d" % c) for c in range(4)]
    gv = [sb("gv%d" % i, [128, 512]) for i in range(2)]
    vhat, _ = sb("vhat", [128, 4, 512], BF16)
    Bvhat = [mkbuf("vhat%d" % c) for c in range(4)]
    gu = [sb("gu%d" % i, [128, 512]) for i in range(2)]
    zz = [sb("zz%d" % i, [128, 512]) for i in range(2)]
    pooledT, _ = sb("pooledT", [128, 4, 512], BF16)
    BpooledT = [mkbuf("pooledT%d" % g) for g in range(4)]
    mixT, _ = sb("mixT", [128, 8, 512], BF16)
    BmixT = [mkbuf("mixT%d" % e) for e in range(8)]
    c1 = [sb("c1_%d" % i, [128, 512]) for i in range(3)]
    gel = [sb("gel%d" % i, [128, 512]) for i in range(2)]
    stt_, _ = sb("stats", [128, 8, 12])
    Bst = [mkbuf("st%d" % i) for i in range(8)]
    mv_, _ = sb("mv", [128, 8, 4])
    Bmv = [mkbuf("mv%d" % i) for i in range(8)]
    stcnt = [0]
    corr, Bcorr = sb("corr", [128, NJ, 2])
    tmp22 = [sb("tmp22_%d" % i, [128, NJ]) for i in range(2)]
    rot = {"gv": 0, "gu": 0, "zz": 0, "c1": 0, "gel": 0}

    def nxt(name, lst):
        i = rot[name] % len(lst)
        rot[name] += 1
        return lst[i]

    stage = [(hid[:].rearrange("p j n -> p (j n)")[:, 0:8192].bitcast(F32).rearrange("p (k n) -> p k n", n=512), mkbuf("stage0")),
             (y[:].rearrange("p c (a n) -> p (c a) n", n=512), mkbuf("stage1"))]
    gbc = [lnt[0][0], lnt[0][1]]
    rowsA, BrowsA = gv[0][0][0:1, :], mkbuf("rowsA")
    rowsB, BrowsB = gv[1][0][0:1, :], mkbuf("rowsB")
    rowsC, BrowsC = gu[0][0][0:1, :], mkbuf("rowsC")
    wsf, Bwsf = gu[1][0][:].rearrange("p (h t) -> p h t", t=128), mkbuf("wsf")
    g8, Bg8 = zz[0][0][0:8, 0:128], mkbuf("g8")

    S.dma("pool", ident_bf[:], ident_d, writes=[Bident])
    S.dma("sp", ident_f[:], ident_d, writes=[Bidentf])
    S.dma("pool", Pprev[:], Pprev_d, writes=[BPprev])
    S.dma("pool", Pcur[:], Pcur_d, writes=[BPcur])
    S.dma("pool", Pfirst[:], Pfirst_d, writes=[BPfirst])
    S.dma("sp", mcol[:], mcol_d, writes=[Bmcol])
    S.op("dve", lambda e: e.memset(ones_f[:], 1.0), writes=[Bonesf])
    S.op("dve", lambda e: e.memset(ones_bf[:], 1.0), writes=[Bonesbf])
    S.op("dve", lambda e: e.memset(mhalf[:], -0.5), writes=[Bmhalf])
    for l in range(nl):
        S.op("dve", lambda e, l=l: e.memset(aprev[l][0][:], 0.0), writes=[aprev[l][1]])
        S.op("dve", lambda e, l=l: e.memset(gcarry[l][0][:], 0.0), writes=[gcarry[l][1]])
    ctmp, Bctmp = sb("ctmp", [128, 8])
    S.dma("sp", ctmp[:], cT_d, writes=[Bctmp])
    S.op("act", lambda e: e.activation(out=cact[:], in_=ctmp[:], func=AF.Silu), reads=[Bctmp], writes=[Bcact])

    def bc_last(ap2, n):
        return ap2.unsqueeze(2).to_broadcast([128, ap2.shape[1], n])

    def bc_mid(ap2, k):
        return ap2.unsqueeze(1).to_broadcast([128, k, ap2.shape[1]])

    def prepass_a(l):
        vtl, Bvt = vt[l]
        S.dma("sp", vtl[:], vecsT_d[l], writes=[Bvt])
        psm, Bpsm = next_bank()
        for i in range(12):
            t, B = ring_next()
            S.dma("pool", t[:], ada_w[l, :, i * 512:(i + 1) * 512].rearrange("(k p) n -> p k n", p=128), writes=[B])
            fns = []
            for j in range(4):
                for k in range(8):
                    fns.append(lambda e, t=t, j=j, k=k, i=i: e.matmul(
                        psm[:, i * 4 + j:i * 4 + j + 1], lhsT=t[:, k, j * 128:(j + 1) * 128], rhs=cact[:, k:k + 1],
                        start=(k == 0), stop=(k == 7)))
            S.group("pe", fns, reads=[B, Bcact], writes=[Bpsm])
        mt, Bmt = modT[l]
        S.op("dve", lambda e: e.tensor_tensor(out=mt[:], in0=psm[:, 0:48], in1=vtl[:, 96:144], op=ALU.add),
             reads=[Bpsm, Bvt], writes=[Bmt])
        g, Bg = gT[l]
        S.op("dve", lambda e: e.tensor_scalar_add(out=g[:, 0:8], in0=mt[:, 8:16], scalar1=1.0), reads=[Bmt], writes=[Bg])
        S.op("dve", lambda e: e.tensor_scalar_add(out=g[:, 8:16], in0=mt[:, 32:40], scalar1=1.0), reads=[Bmt], writes=[Bg])
        b2, Bb2 = bT2[l]
        S.op("dve", lambda e: e.tensor_tensor(out=b2[:], in0=vtl[:, 152:160], in1=g[:, 8:16], op=ALU.mult), reads=[Bvt, Bg], writes=[Bb2])
        S.op("dve", lambda e: e.tensor_tensor(out=b2[:], in0=b2[:], in1=mt[:, 24:32], op=ALU.add), reads=[Bmt, Bb2], writes=[Bb2])
        S.op("dve", lambda e: e.tensor_tensor(out=g[:, 8:16], in0=g[:, 8:16], in1=vtl[:, 144:152], op=ALU.mult), reads=[Bvt, Bg], writes=[Bg])
        stg = lnt[1][0]
        S.dma("sp", stg[0][0:1, :], ln1_b[l:l + 1, :], writes=[stg[1]])
        S.op("dve", lambda e: e.tensor_copy(out=yb[0:1, 0, :], in_=stg[0][0:1, :]), reads=[stg[1]], writes=[Byb[0]])
        S.op("dve", lambda e: e.tensor_tensor(out=yb[0:1, 1, :], in0=stg[0][0:1, :], in1=yb[0:1, 0, :], op=ALU.subtract),
             reads=[stg[1], Byb[0]], writes=[Byb[1]])
        S.dma("sp", b1sc[l:l + 1, :].rearrange("o (a n) -> o a n", a=2), yb[0:1, 0:2, :], reads=[Byb[0], Byb[1]], writes=[Bb1sc[l]], owner=Byb[0])

    def prepass_w(l):
        vtl, Bvt = vt[l]
        mt, Bmt = modT[l]
        g, Bg = gT[l]
        b2, Bb2 = bT2[l]
        for s in range(2):
            pt, Bpt = next_bank()
            col = 16 + 24 * s
            S.op("pe", lambda e, pt=pt, col=col: e.transpose(pt[0:8, 0:128], mt[:, col:col + 8], ident_f[:]),
                 reads=[Bmt, Bidentf], writes=[Bpt])
            S.op("dve", lambda e, pt=pt: e.tensor_copy(out=g8[:], in_=pt[0:8, 0:128]), reads=[Bpt], writes=[Bg8])
            S.dma("sp", gsc[l, s].rearrange("(k p) -> k p", p=128), g8[:], reads=[Bg8], writes=[Bgsc[l][s]], owner=Bg8)
            S.dma("sp", gbc[s][0][:], gsc[l, s].partition_broadcast(128), reads=[Bgsc[l][s]], writes=[gbc[s][1]])
            if s == 1:
                S.op("dve", lambda e: e.tensor_scalar_mul(out=gbc[1][0][:], in0=gbc[1][0][:], scalar1=1.0 / ALPHA),
                     reads=[gbc[1][1]], writes=[gbc[1][1]])
        psr, Bpsr = next_bank()
        sidx = [0]

        def get_stage():
            i = sidx[0] % 2
            sidx[0] += 1
            return stage[i]

        def emit_scaled(st, Bs, s, nk, scale_ap, Bscale, eng, rowcols=None):
            t, B = ring_next()
            if eng == "act":
                fns = [lambda e, k=k: e.activation(out=t[:, k, :], in_=st[:, k, :], func=AF.Identity, scale=g[:, rowcols + k:rowcols + k + 1])
                       for k in range(nk)]
                S.group("act", fns, reads=[Bs, Bscale], writes=[B])
            else:
                S.op("dve", lambda e: e.tensor_tensor(out=t[:, 0:nk, :], in0=st[:, 0:nk, :], in1=scale_ap, op=ALU.mult),
                     reads=[Bs, Bscale], writes=[B])
            S.dma("act" if eng != "act" else "sp", wsc[l, s, :, 0:nk * 512].rearrange("p (k n) -> p k n", n=512), t[:, 0:nk, :],
                  reads=[B], writes=[Bwsc[l][s]], owner=B)

        def fm_bias(st, Bs, off, shcol, outcol):
            src, Bsrc, c0 = (mt, Bmt, shcol) if shcol == 0 else (b2, Bb2, 0)
            fns = []
            for k in range(8):
                fns.append(lambda e, k=k: e.matmul(psr[:, outcol:outcol + 1], lhsT=st[:, k, off:off + 128],
                                                   rhs=src[:, c0 + k:c0 + k + 1], start=(k == 0), stop=(k == 7)))
            S.group("pe", fns, reads=[Bs, Bsrc], writes=[Bpsr])

        engs = ["dve", "act"]
        for s, c0 in ((0, 0), (1, 1024), (2, 512)):
            st, Bs = get_stage()
            S.dma("sp", st[:], w_in[l, :, c0:c0 + 512].rearrange("(k p) n -> p k n", p=128), writes=[Bs])
            if s == 1:
                pr, Bpr = next_bank()
                fns = [lambda e, k=k: e.matmul(pr[0:1, :], lhsT=mt[:, k:k + 1], rhs=st[:, k, :],
                                               start=(k == 0), stop=(k == 7)) for k in range(8)]
                S.group("pe", fns, reads=[Bs, Bmt], writes=[Bpr])
                rv, Brv = rvrow[l]
                S.op("act", lambda e: e.copy(out=rv[0:1, 0:512], in_=pr[0:1, :]), reads=[Bpr], writes=[Brv])
                S.op("dve", lambda e: e.tensor_tensor(out=rv[0:1, 512:1024], in0=pr[0:1, :], in1=rv[0:1, 0:512],
                                                      op=ALU.subtract), reads=[Bpr, Brv], writes=[Brv])
            if s == 2:
                for j in range(4):
                    fm_bias(st, Bs, j * 128, 0, 44 + j)
            emit_scaled(st, Bs, s, 8, bc_last(g[:, 0:8], 512), Bg, engs[s % 2], 0)
        for hf in range(2):
            st, Bs = get_stage()
            S.dma("sp", st[:], w_out[l, :, hf * 512:(hf + 1) * 512].rearrange("(k p) n -> p k n", p=128), writes=[Bs])
            emit_scaled(st, Bs, 3 + hf, 8, bc_mid(gbc[0][0][:, hf * 512:(hf + 1) * 512], 8), gbc[0][1], "dve")
        for i in range(11):
            j0 = 2 * i
            st, Bs = get_stage()
            S.dma2("sp", [(st[:, :, 0:256], w_up[l, :, j0 * 128:j0 * 128 + 256].rearrange("(k p) n -> p k n", p=128)),
                          (st[:, :, 256:512], w_up[l, :, DFF + j0 * 128:DFF + j0 * 128 + 256].rearrange("(k p) n -> p k n", p=128))],
                   writes=[Bs])
            for fc in range(2):
                fm_bias(st, Bs, fc * 128, 24, j0 + fc)
                fm_bias(st, Bs, 256 + fc * 128, 24, 22 + j0 + fc)
            emit_scaled(st, Bs, 5 + i, 8, bc_last(g[:, 8:16], 512), Bg, engs[i % 2], 8)
        for hf in range(2):
            for part, (ja, jb) in enumerate(((0, 8), (8, 16), (16, 22))):
                nk = jb - ja
                st, Bs = get_stage()
                S.dma("sp", st[:, 0:nk, :], w_down[l, ja * 128:jb * 128, hf * 512:(hf + 1) * 512].rearrange("(k p) n -> p k n", p=128), writes=[Bs])
                emit_scaled(st, Bs, 16 + hf * 3 + part, nk, bc_mid(gbc[1][0][:, hf * 512:(hf + 1) * 512], nk), gbc[1][1], "dve")
        ra, Bra = rall[l]
        S.op("dve", lambda e: e.tensor_copy(out=ra[:], in_=psr[:, 0:48]), reads=[Bpsr], writes=[Bra])
        cb, Bcb = cbp[l]
        S.op("dve", lambda e: e.tensor_tensor(out=cb[:], in0=vtl[:, 8:30], in1=vtl[:, 30:52], op=ALU.add), reads=[Bvt], writes=[Bcb])
        S.op("dve", lambda e: e.tensor_tensor(out=cb[:], in0=cb[:], in1=vtl[:, 52:74], op=ALU.add), reads=[Bvt, Bcb], writes=[Bcb])
        S.op("dve", lambda e: e.tensor_tensor(out=cb[:], in0=cb[:], in1=ra[:, 0:22], op=ALU.mult), reads=[Bra, Bcb], writes=[Bcb])
        S.op("dve", lambda e: e.tensor_tensor(out=cb[:], in0=cb[:], in1=vtl[:, 74:96], op=ALU.add), reads=[Bvt, Bcb], writes=[Bcb])
        S.dma("sp", wsf[:], sgu_wT[l].rearrange("h s t -> s h t"), writes=[Bwsf])
        mk, Bmk = stage[0]
        S.dma("sp", mk[:, 0, 0:128], maskT_d, writes=[Bmk])
        S.op("dve", lambda e: e.tensor_tensor(out=wsf[:], in0=wsf[:], in1=bc_mid(mk[:, 0, 0:128], 4), op=ALU.mult),
             reads=[Bwsf, Bmk], writes=[Bwsf])
        S.op("dve", lambda e: e.tensor_copy(out=WsT[l][0][:], in_=wsf[:]), reads=[Bwsf], writes=[WsT[l][1]])
        prs, Bprs = next_bank()
        S.op("pe", lambda e: e.matmul(prs[0:1, :], lhsT=ones_f[:, 0:1], rhs=wsf[:].rearrange("p h t -> p (h t)"),
                                      start=True, stop=True), reads=[Bwsf, Bonesf], writes=[Bprs])
        S.op("dve", lambda e: e.tensor_copy(out=rowsA[:], in_=prs[0:1, :]), reads=[Bprs], writes=[BrowsA])
        S.dma("sp", rowsB[:], sgu_ln_b[l:l + 1, :], writes=[BrowsB])
        S.dma("sp", rowsC[:], sgu_b[l:l + 1, :], writes=[BrowsC])
        pc, Bpc = next_bank()
        fns = []
        for h in range(4):
            hs = slice(h * 128, (h + 1) * 128)
            fns.append(lambda e, hs=hs: e.matmul(pc[:, hs], lhsT=rowsB[0:1, hs], rhs=rowsA[0:1, hs], start=True, stop=False))
            fns.append(lambda e, hs=hs: e.matmul(pc[:, hs], lhsT=ones_f[0:1, 0:128], rhs=rowsC[0:1, hs], start=False, stop=True))
        S.group("pe", fns, reads=[BrowsA, BrowsB, BrowsC, Bonesf], writes=[Bpc])
        S.op("dve", lambda e: e.tensor_copy(out=Cc[l][0][:].rearrange("p h t -> p (h t)"), in_=pc[:]), reads=[Bpc], writes=[Cc[l][1]])
        S.dma("pool", poolw[l][0][:], pool_w[l].rearrange("g c d -> c g d"), writes=[poolw[l][1]])

    def transposes(c_lo, nch):
        for ci in range(nch):
            tp, Btp = next_tp()
            fns = [lambda e, tp=tp, k=k, ci=ci: e.transpose(tp[:, k, :], yb[:, c_lo + ci, k * 128:(k + 1) * 128], ident_bf[:])
                   for k in range(8)]
            S.group("pe", fns, reads=[Byb[c_lo + ci], Bident], writes=[Btp])
            if ci % 2 == 0:
                S.op("dve", lambda e, tp=tp, ci=ci: e.tensor_copy(out=hT[:, :, ci * 128:(ci + 1) * 128], in_=tp[:]),
                     reads=[Btp], writes=[BhT[ci]])
            else:
                S.op("act", lambda e, tp=tp, ci=ci: e.copy(out=hT[:, :, ci * 128:(ci + 1) * 128], in_=tp[:]),
                     reads=[Btp], writes=[BhT[ci]])

    def ln_stats(src_ap_list, Bsrc, eps=EPS):
        i = stcnt[0] % 8
        stcnt[0] += 1
        n = len(src_ap_list)
        for q, ap in enumerate(src_ap_list):
            S.op("dve", lambda e, ap=ap, q=q: e.bn_stats(out=stt_[:, i, q * 6:(q + 1) * 6], in_=ap), reads=[Bsrc], writes=[Bst[i]])
        S.op("dve", lambda e: e.bn_aggr(out=mv_[:, i, 0:2], in_=stt_[:, i, 0:6 * n]), reads=[Bst[i]], writes=[Bmv[i]])
        S.op("dve", lambda e: e.tensor_scalar_add(out=mv_[:, i, 1:2], in0=mv_[:, i, 1:2], scalar1=eps), reads=[Bmv[i]], writes=[Bmv[i]])
        S.op("pool", lambda e: e.tensor_tensor(out=mv_[:, i, 2:3], in0=mv_[:, i, 1:2], in1=mhalf[:], op=ALU.pow),
             reads=[Bmv[i], Bmhalf], writes=[Bmv[i]])
        S.op("dve", lambda e: e.scalar_tensor_tensor(out=mv_[:, i, 3:4], in0=mv_[:, i, 0:1], scalar=-1.0, in1=mv_[:, i, 2:3],
                                                     op0=ALU.mult, op1=ALU.mult), reads=[Bmv[i]], writes=[Bmv[i]])
        return mv_[:, i, 0:1], mv_[:, i, 2:3], mv_[:, i, 3:4], Bmv[i]

    def resid_ln_a(c, ps_halves):
        for hf, (ps, Bps) in enumerate(ps_halves):
            sl = slice(hf * 512, (hf + 1) * 512)
            S.op("dve", lambda e, ps=ps, sl=sl: e.scalar_tensor_tensor(out=y[:, c, sl], in0=y[:, c, sl], scalar=ALPHA, in1=ps[:],
                                                                     op0=ALU.mult, op1=ALU.add), reads=[Bps, By[c]], writes=[By[c]])
        return ln_stats([y[:, c, 0:512], y[:, c, 512:1024]], By[c])

    def resid_ln1_b(c, st):
        mean, rstd, nmr, Bm = st
        S.op("act", lambda e: e.activation(out=yb[:, c, :], in_=y[:, c, :], func=AF.Identity, scale=rstd, bias=nmr),
             reads=[By[c], Bm], writes=[Byb[c]])
        S.op("act", lambda e: e.activation(out=y[:, c, :], in_=y[:, c, :], func=AF.Identity, scale=rstd, bias=nmr),
             reads=[By[c], Bm], writes=[By[c]])

    def resid_ln2_a(c, ps_halves, g1t):
        S.op("dve", lambda e: e.tensor_tensor(out=y[:, c, :], in0=y[:, c, :], in1=g1t[0][:], op=ALU.mult),
             reads=[By[c], g1t[1]], writes=[By[c]])
        for hf, (ps, Bps) in enumerate(ps_halves):
            sl = slice(hf * 512, (hf + 1) * 512)
            S.op("dve", lambda e, ps=ps, sl=sl: e.tensor_tensor(out=y[:, c, sl], in0=y[:, c, sl], in1=ps[:], op=ALU.add),
                 reads=[Bps, By[c]], writes=[By[c]])
        return ln_stats([y[:, c, 0:512], y[:, c, 512:1024]], By[c], EPS / (ALPHA * ALPHA))

    def resid_ln_b(c, st, lng, lnb):
        mean, rstd, nmr, Bm = st
        S.op("dve", lambda e: e.scalar_tensor_tensor(out=y[:, c, :], in0=y[:, c, :], scalar=mean, in1=lng[0][:],
                                                     op0=ALU.subtract, op1=ALU.mult), reads=[By[c], Bm, lng[1]], writes=[By[c]])
        S.op("dve", lambda e: e.scalar_tensor_tensor(out=y[:, c, :], in0=y[:, c, :], scalar=rstd, in1=lnb[0][:],
                                                     op0=ALU.mult, op1=ALU.add), reads=[By[c], Bm, lnb[1]], writes=[By[c]])
        if not state["skip_yb"]:
            S.op("act", lambda e: e.copy(out=yb[:, c, :], in_=y[:, c, :]), reads=[By[c]], writes=[Byb[c]])

    def mixer(l, c_lo, nch, first_own):
        T = nch * 128
        vtl, Bvt = vt[l]
        ra, Bra = rall[l]
        ap_, Bap = aprev[l]
        if first_own:
            S.op("dve", lambda e: e.tensor_scalar_mul(out=ap_[:], in0=ap_[:], scalar1=mcol[:, 0:1]), reads=[Bap, Bmcol], writes=[Bap])
        S.dma("sp", lnt[0][0][0][:], ln1_g[l].partition_broadcast(128), writes=[lnt[0][0][1]])
        transposes(c_lo, nch)
        hTall = BhT
        sa, Bsa = ring_load(l, 0)
        for ci in range(nch):
            ps, Bps = next_bank()
            fns = [lambda e, ps=ps, k=k, ci=ci: e.matmul(ps[:], lhsT=hT[:, k, ci * 128:(ci + 1) * 128], rhs=sa[:, k, :],
                                                        start=(k == 0), stop=(k == 7)) for k in range(8)]
            S.group("pe", fns, reads=[BhT[ci], Bsa], writes=[Bps])
            S.op("act", lambda e, ps=ps, ci=ci: e.copy(out=acur[:, ci, :], in_=ps[:]), reads=[Bps], writes=[Bacur[ci]])
        sv, Bsv = ring_load(l, 1)
        rv, Brv = rvrow[l]
        for ci in range(nch):
            ps, Bps = next_bank()
            fns = [lambda e, ps=ps, k=k, ci=ci: e.matmul(ps[:], lhsT=hT[:, k, ci * 128:(ci + 1) * 128], rhs=sv[:, k, :],
                                                        start=(k == 0), stop=False) for k in range(8)]
            fns.append(lambda e, ps=ps: e.matmul(ps[:], lhsT=ones_bf[0:1, 0:128], rhs=rv[0:1, 0:512], start=False, stop=False))
            fns.append(lambda e, ps=ps: e.matmul(ps[:], lhsT=ones_bf[0:1, 0:128], rhs=rv[0:1, 512:1024], start=False, stop=True))
            S.group("pe", fns, reads=[BhT[ci], Bsv, Brv, Bonesbf], writes=[Bps])
            g_, Bg_ = nxt("gv", gv)
            S.op("act", lambda e, ps=ps, g_=g_: e.activation(out=g_[:], in_=ps[:], func=AF.Gelu), reads=[Bps], writes=[Bg_])
            mean, rstd, nmr, Bm = ln_stats([g_[:]], Bg_)
            S.op("pool", lambda e, g_=g_, ci=ci, rstd=rstd, nmr=nmr: e.tensor_scalar(
                out=vhat[:, ci, :], in0=g_[:], scalar1=rstd, scalar2=nmr, op0=ALU.mult, op1=ALU.add),
                reads=[Bg_, Bm], writes=[Bvhat[ci]])
        pw, Bpw = poolw[l]
        for g in range(4):
            ps, Bps = next_bank()
            fns = []
            gs = slice(g * 128, (g + 1) * 128)
            for ci in range(nch):
                prev_ap = ap_[:, gs] if ci == 0 else acur[:, ci - 1, gs]
                Pc = Pfirst if (first_own and ci == 0) else Pcur
                fns.append(lambda e, ps=ps, ci=ci, prev_ap=prev_ap: e.matmul(ps[:, ci * 128:(ci + 1) * 128], lhsT=prev_ap,
                                                                             rhs=Pprev[:, g, :], start=True, stop=False))
                fns.append(lambda e, ps=ps, ci=ci, Pc=Pc: e.matmul(ps[:, ci * 128:(ci + 1) * 128], lhsT=acur[:, ci, gs],
                                                                   rhs=Pc[:, g, :], start=False, stop=True))
            S.group("pe", fns, reads=[Bap, BPprev, BPcur, BPfirst] + Bacur[:nch], writes=[Bps])
            S.op("dve", lambda e, ps=ps, g=g: e.tensor_copy(out=pooledT[:, g, 0:T], in_=ps[:, 0:T]), reads=[Bps], writes=[BpooledT[g]])
        S.op("pool", lambda e: e.tensor_copy(out=ap_[:], in_=acur[:, nch - 1, :]), reads=[Bacur[nch - 1]], writes=[Bap])
        for g in range(4):
            ps, Bps = next_bank()
            S.op("pe", lambda e, ps=ps, g=g: e.matmul(ps[:, 0:T], lhsT=pw[:, g, :], rhs=pooledT[:, g, 0:T], start=True, stop=True),
                 reads=[Bpw, BpooledT[g]], writes=[Bps])
            S.op("act", lambda e, ps=ps, g=g: e.activation(out=mixT[:, g, 0:T], in_=ps[:, 0:T], func=AF.Identity, scale=vtl[:, g:g + 1]),
                 reads=[Bps, Bvt], writes=[BmixT[g]])
        su, Bsu = ring_load(l, 2)
        wst, Bwst = WsT[l]
        cc, Bcc = Cc[l]
        pzs, pus = [], []
        for h in range(4):
            hs = slice(h * 128, (h + 1) * 128)
            pz, Bpz = next_bank()
            fns = [lambda e, pz=pz, ci=ci: e.matmul(pz[:, ci * 128:(ci + 1) * 128], lhsT=vhat[:, ci, hs], rhs=wst[:, h, :],
                                                   start=True, stop=True) for ci in range(nch)]
            S.group("pe", fns, reads=Bvhat[:nch] + [Bwst], writes=[Bpz])
            pzs.append((pz, Bpz))
            if h < 2:
                zz_, Bzz = zz[h]
                S.op("dve", lambda e, pz=pz, zz_=zz_: e.scalar_tensor_tensor(
                    out=zz_[:, 0:T].rearrange("p (c t) -> p c t", t=128), in0=pz[:, 0:T].rearrange("p (c t) -> p c t", t=128),
                    scalar=vtl[:, 4 + h:5 + h], in1=cc[:, h, :].unsqueeze(1).to_broadcast([128, nch, 128]),
                    op0=ALU.mult, op1=ALU.add), reads=[Bpz, Bvt, Bcc], writes=[Bzz])
        for h in range(4):
            hs = slice(h * 128, (h + 1) * 128)
            ps, Bps = next_bank()
            fns = [lambda e, ps=ps, k=k: e.matmul(ps[:, 0:T], lhsT=su[:, k, hs], rhs=hT[:, k, 0:T], start=(k == 0), stop=(k == 7))
                   for k in range(8)]
            S.group("pe", fns, reads=hTall + [Bsu], writes=[Bps])
            pus.append((ps, Bps))
        for h in range(4):
            ps, Bps = pus[h]
            gu_, Bgu = nxt("gu", gu)
            S.op("act", lambda e, ps=ps, gu_=gu_: e.activation(out=gu_[:, 0:T], in_=ps[:, 0:T], func=AF.Gelu, bias=ra[:, 44 + h:45 + h]),
                 reads=[Bps, Bra], writes=[Bgu])
            zz_, Bzz = zz[h % 2]
            if h >= 2:
                pz, Bpz = pzs[h]
                S.op("dve", lambda e, pz=pz, zz_=zz_: e.scalar_tensor_tensor(
                    out=zz_[:, 0:T].rearrange("p (c t) -> p c t", t=128), in0=pz[:, 0:T].rearrange("p (c t) -> p c t", t=128),
                    scalar=vtl[:, 4 + h:5 + h], in1=cc[:, h, :].unsqueeze(1).to_broadcast([128, nch, 128]),
                    op0=ALU.mult, op1=ALU.add), reads=[Bpz, Bvt, Bcc], writes=[Bzz])
            S.op("dve", lambda e, zz_=zz_, gu_=gu_: e.tensor_tensor(out=mixT[:, 4 + h, 0:T], in0=gu_[:, 0:T], in1=zz_[:, 0:T], op=ALU.mult),
                 reads=[Bgu, Bzz], writes=[BmixT[4 + h]])
        so = [ring_load(l, 3), ring_load(l, 4)]
        if state["ycopy"]:
            for c in range(4):
                S.op("dve", lambda e, c=c: e.tensor_copy(out=y[:, c, :], in_=xpre[:, c, :]), reads=Bhid, writes=[By[c]])
            state["ycopy"] = False
        pend = None
        for ci in range(nch):
            halves = []
            for hf in range(2):
                ps, Bps = next_bank()
                w_, Bw_ = so[hf]
                fns = [lambda e, ps=ps, k=k, w_=w_: e.matmul(ps[:], lhsT=mixT[:, k, ci * 128:(ci + 1) * 128], rhs=w_[:, k, :],
                                                            start=(k == 0), stop=(k == 7)) for k in range(8)]
                S.group("pe", fns, reads=BmixT + [Bw_], writes=[Bps])
                halves.append((ps, Bps))
            if pend is not None and ci == nch - 1:
                resid_ln1_b(pend[0], pend[1])
                pend = None
            st = resid_ln_a(c_lo + ci, halves)
            if pend is not None:
                resid_ln1_b(pend[0], pend[1])
            pend = (c_lo + ci, st)
        resid_ln1_b(pend[0], pend[1])

    def ffn(l, c_lo, nch, first_own):
        T = nch * 128
        vtl, Bvt = vt[l]
        ra, Bra = rall[l]
        cb, Bcb = cbp[l]
        gc, Bgc = gcarry[l]
        if first_own:
            rgb = ra[:, 0:22].unsqueeze(2).to_broadcast([128, NJ, 2])
            S.op("dve", lambda e: e.tensor_tensor(out=gc[:], in0=gc[:], in1=rgb, op=ALU.add), reads=[Bgc, Bra], writes=[Bgc])
            S.op("dve", lambda e: e.tensor_scalar_mul(out=gc[:], in0=gc[:], scalar1=mcol[:, 0:1]), reads=[Bgc, Bmcol], writes=[Bgc])
            S.op("dve", lambda e: e.tensor_tensor(out=gc[:], in0=gc[:], in1=rgb, op=ALU.subtract), reads=[Bgc, Bra], writes=[Bgc])
        lng, lnb = lnt[1]
        rvt, Brvt = lnt[0][1][0][0:1, :].bitcast(BF16), lnt[0][1][1]
        S.dma("sp", rvt, b1sc[l:l + 1, :], reads=[Bb1sc[l]], writes=[Brvt])
        S.dma("sp", lng[0][:], ln2_g[l].partition_broadcast(128), writes=[lng[1]])
        S.dma("sp", lnb[0][:], ln2_b[l].partition_broadcast(128), writes=[lnb[1]])
        transposes(c_lo, nch)
        t0, Bt0 = tmp22[0]
        S.op("dve", lambda e: e.tensor_tensor(out=corr[:, :, 1], in0=gc[:, :, 1], in1=vtl[:, 8:30], op=ALU.mult), reads=[Bgc, Bvt], writes=[Bcorr])
        S.op("dve", lambda e: e.tensor_tensor(out=t0[:], in0=gc[:, :, 1], in1=vtl[:, 30:52], op=ALU.mult), reads=[Bgc, Bvt], writes=[Bt0])
        S.op("dve", lambda e: e.tensor_tensor(out=corr[:, :, 0], in0=gc[:, :, 0], in1=vtl[:, 8:30], op=ALU.mult), reads=[Bgc, Bvt], writes=[Bcorr])
        S.op("dve", lambda e: e.tensor_tensor(out=corr[:, :, 0], in0=corr[:, :, 0], in1=t0[:], op=ALU.add), reads=[Bt0, Bcorr], writes=[Bcorr])
        for i in range(11):
            su, Bsu = ring_load(l, 5 + i)
            for fc in range(2):
                j = 2 * i + fc
                pg, Bpg = next_bank()
                pv, Bpv = next_bank()
                fns = [lambda e, pg=pg, k=k: e.matmul(pg[:, 0:T], lhsT=su[:, k, fc * 128:(fc + 1) * 128], rhs=hT[:, k, 0:T],
                                                     start=(k == 0), stop=(k == 7)) for k in range(8)]
                S.group("pe", fns, reads=BhT + [Bsu], writes=[Bpg])
                fns = [lambda e, pv=pv, k=k: e.matmul(pv[:, 0:T], lhsT=su[:, k, 256 + fc * 128:256 + (fc + 1) * 128], rhs=hT[:, k, 0:T],
                                                     start=(k == 0), stop=(k == 7)) for k in range(8)]
                S.group("pe", fns, reads=BhT + [Bsu], writes=[Bpv])
                c_, Bc_ = nxt("c1", c1)
                S.op("act", lambda e, pg=pg, c_=c_: e.activation(out=c_[:, 0:T], in_=pg[:, 0:T], func=AF.Identity,
                                                                scale=vtl[:, 52 + j:53 + j], bias=cb[:, j:j + 1]),
                     reads=[Bpg, Bvt, Bcb], writes=[Bc_])
                S.op("dve", lambda e, c_=c_: e.tensor_tensor(out=c_[:, 0:2], in0=c_[:, 0:2], in1=corr[:, j, :], op=ALU.add),
                     reads=[Bc_, Bcorr], writes=[Bc_])
                S.op("dve", lambda e, pg=pg, c_=c_: e.scalar_tensor_tensor(out=c_[:, 1:T], in0=pg[:, 0:T - 1], scalar=vtl[:, 30 + j:31 + j],
                                                                          in1=c_[:, 1:T], op0=ALU.mult, op1=ALU.add),
                     reads=[Bpg, Bc_, Bvt], writes=[Bc_])
                S.op("dve", lambda e, pg=pg, c_=c_: e.scalar_tensor_tensor(out=c_[:, 2:T], in0=pg[:, 0:T - 2], scalar=vtl[:, 8 + j:9 + j],
                                                                          in1=c_[:, 2:T], op0=ALU.mult, op1=ALU.add),
                     reads=[Bpg, Bc_, Bvt], writes=[Bc_])
                S.op("act", lambda e, pg=pg: e.copy(out=gc[:, j, :], in_=pg[:, T - 2:T]), reads=[Bpg, Bcorr], writes=[Bgc])
                ge, Bge = nxt("gel", gel)
                S.op("act", lambda e, c_=c_, ge=ge: e.activation(out=ge[:, 0:T], in_=c_[:, 0:T], func=AF.Gelu), reads=[Bc_], writes=[Bge])
                S.op("dve", lambda e, pv=pv, ge=ge: e.scalar_tensor_tensor(out=hid[:, j, 0:T], in0=pv[:, 0:T], scalar=ra[:, 22 + j:23 + j],
                                                                          in1=ge[:, 0:T], op0=ALU.add, op1=ALU.mult),
                     reads=[Bpv, Bge, Bra], writes=[Bhid[j]])
        sd = [[ring_load(l, 16 + hf * 3 + p, nk) for p, nk in enumerate((8, 8, 6))] for hf in range(2)]
        pend = None
        for ci in range(nch):
            halves = []
            for hf in range(2):
                ps, Bps = next_bank()
                fns = []
                for j in range(NJ):
                    w_ = sd[hf][j // 8][0]
                    fns.append(lambda e, ps=ps, j=j, w_=w_: e.matmul(ps[:], lhsT=hid[:, j, ci * 128:(ci + 1) * 128], rhs=w_[:, j % 8, :],
                                                                    start=(j == 0), stop=(j == NJ - 1)))
                fns[-1] = (lambda e, ps=ps, w_=sd[hf][2][0]: e.matmul(ps[:], lhsT=hid[:, NJ - 1, ci * 128:(ci + 1) * 128], rhs=w_[:, (NJ - 1) % 8, :],
                                                                     start=False, stop=False))
                cs = slice(hf * 512, (hf + 1) * 512)
                fns.append(lambda e, ps=ps, cs=cs: e.matmul(ps[:], lhsT=ones_bf[0:1, 0:128], rhs=rvt[0:1, cs], start=False, stop=False))
                fns.append(lambda e, ps=ps, cs=cs: e.matmul(ps[:], lhsT=ones_bf[0:1, 0:128], rhs=rvt[0:1, D + hf * 512:D + (hf + 1) * 512], start=False, stop=True))
                S.group("pe", fns[0:16], reads=Bhid[0:16] + [b for _, b in sd[hf]], writes=[Bps])
                S.group("pe", fns[16:], reads=Bhid[16:] + [b for _, b in sd[hf]] + [Brvt, Bonesbf], writes=[Bps])
                halves.append((ps, Bps))
            if pend is not None and ci == nch - 1:
                resid_ln_b(pend[0], pend[1], lng, lnb)
                pend = None
            if ci == nch - 1 and state["prefetch"] is not None:
                tn = state["prefetch"]
                S.dma("sp", xpre, xin[tn * 512:(tn + 1) * 512, :].rearrange("(c p) d -> p c d", p=128), writes=Bhid)
                state["prefetch"] = None
            st = resid_ln2_a(c_lo + ci, halves, lnt[0][0])
            if pend is not None:
                resid_ln_b(pend[0], pend[1], lng, lnb)
            pend = (c_lo + ci, st)
        resid_ln_b(pend[0], pend[1], lng, lnb)

    state = {"skip_yb": False, "ycopy": False, "prefetch": None}
    xpre = hid[:].rearrange("p j n -> p (j n)")[:, 0:8192].bitcast(F32).rearrange("p (c n) -> p c n", n=D)

    prepass_a(0)
    for l in range(nl):
        if l + 1 < nl:
            prepass_a(l + 1)
        prepass_w(l)
    S.barrier(allbufs)

    for t in range(ntiles):
        if t == 0:
            for c in range(4):
                S.dma("pool", y[:, c, :], xin[(t * 4 + c) * 128:(t * 4 + c + 1) * 128, :], writes=[By[c]])
                S.op("act", lambda e, c=c: e.copy(out=yb[:, c, :], in_=y[:, c, :]), reads=[By[c]], writes=[Byb[c]])
        else:
            for c in range(4):
                S.op("act", lambda e, c=c: e.copy(out=yb[:, c, :], in_=xpre[:, c, :]), reads=Bhid, writes=[Byb[c]])
            state["ycopy"] = True
        for l in range(nl):
            c_lo = first_chunks[l] if t == 0 else 0
            if t == 0 and c_lo >= 4:
                continue
            nch = 4 - c_lo
            last = (l == nl - 1)
            mixer(l, c_lo, nch, t == 1)
            if last and t + 1 < ntiles:
                state["prefetch"] = t + 1
            state["skip_yb"] = last
            ffn(l, c_lo, nch, t == 1)
            state["skip_yb"] = False
        for c in range(4):
            if out_all:
                row = (t * 4 + c) * 128
            elif t == 0:
                continue
            else:
                row = ((t - 1) * 4 + c) * 128
            S.dma("pool", yout[row:row + 128, :], y[:, c, :], reads=[By[c]], owner=By[c])
    S.barrier(allbufs)
    return nc


def _pool_mats():
    Pprev = np.zeros((4, 128, 128), np.float32)
    Pcur = np.zeros((4, 128, 128), np.float32)
    Pfirst = np.zeros((4, 128, 128), np.float32)
    t = np.arange(128)[None, :]
    s = np.arange(128)[:, None]
    for g, w in enumerate(WINDOWS):
        Pcur[g] = np.where((s <= t) & (s >= t - w + 1), 1.0 / w, 0.0) - np.where(s == t, 1.0, 0.0)
        Pprev[g] = np.where((s - 128) >= t - w + 1, 1.0 / w, 0.0)
        cnt = np.minimum(t + 1, w).astype(np.float32)
        Pfirst[g] = np.where((s <= t) & (s >= t - w + 1), 1.0 / cnt, 0.0) - np.where(s == t, 1.0, 0.0)
    tr = lambda a: np.ascontiguousarray(a.transpose(1, 0, 2))
    return tr(Pprev), tr(Pcur), tr(Pfirst)


_PROG = {}


def _get_prog(nl, first_chunks, ntiles, out_all):
    key = (nl, tuple(first_chunks), ntiles, out_all)
    if key not in _PROG:
        _PROG[key] = build(nl, first_chunks, ntiles, out_all)
    return _PROG[key]


def run_model(inputs, fused=True, cores_per_seq=4, n_cores=8, nlayers=DEPTH):
    x = np.asarray(inputs["x"], np.float32)
    c = np.asarray(inputs["c"], np.float32)
    B, SEQ, _ = x.shape
    own = SEQ // cores_per_seq
    ntiles = own // 512 + 1
    f32 = lambda k: np.ascontiguousarray(np.asarray(inputs[k], np.float32))
    Pprev, Pcur, Pfirst = _pool_mats()
    s_ = np.arange(128)[:, None]
    t_ = np.arange(128)[None, :]
    maskT = (s_ <= t_).astype(np.float32)
    ident = np.eye(128, dtype=np.float32)
    L = nlayers
    vecsT = np.zeros((L, 128, 160), np.float32)
    for l in range(L):
        cols = np.concatenate([
            f32("pool_scale")[l].reshape(4, 128), f32("sgu_ln_g")[l].reshape(4, 128),
            f32("conv_w")[l, 0].reshape(NJ, 128), f32("conv_w")[l, 1].reshape(NJ, 128), f32("conv_w")[l, 2].reshape(NJ, 128),
            f32("conv_b")[l].reshape(NJ, 128), f32("ada_b")[l].reshape(48, 128),
            f32("ln1_g")[l].reshape(8, 128), f32("ln1_b")[l].reshape(8, 128)], axis=0)
        vecsT[l] = cols.T
    sgu_wT = np.ascontiguousarray(f32("sgu_w").transpose(0, 1, 3, 2))
    shared = {
        "Pprev": Pprev, "Pcur": Pcur, "maskT": maskT, "ident": ident,
        "ada_w": f32("ada_w"), "vecsT": vecsT, "w_in": f32("w_in"), "pool_w": f32("pool_w"), "sgu_wT": sgu_wT,
        "sgu_ln_b": f32("sgu_ln_b"), "sgu_b": f32("sgu_b").reshape(-1, 512), "w_out": f32("w_out"),
        "ln1_g": f32("ln1_g"), "ln1_b": f32("ln1_b"), "w_up": f32("w_up"), "w_down": f32("w_down"),
        "ln2_g": f32("ln2_g"), "ln2_b": f32("ln2_b"),
    }
    layer_keys = ["ada_w", "vecsT", "w_in", "pool_w", "sgu_wT", "sgu_ln_b", "sgu_b", "w_out", "ln1_g", "ln1_b",
                  "w_up", "w_down", "ln2_g", "ln2_b"]
    percore = []
    for i in range(n_cores):
        b, q = divmod(i, cores_per_seq)
        if q == 0:
            halo = np.zeros((512, D), np.float32)
        else:
            halo = x[b, q * own - 512:q * own]
        xi = np.concatenate([halo, x[b, q * own:(q + 1) * own]], axis=0)
        percore.append({
            "xin": np.ascontiguousarray(xi),
            "cT": np.ascontiguousarray(c[b].reshape(8, 128).T),
            "mcol": np.full((128, 1), 0.0 if q == 0 else 1.0, np.float32),
            "Pfirst": Pfirst if q == 0 else Pcur,
        })
    out = np.zeros((B, SEQ, D), np.float32)
    if fused:
        nc = _get_prog(L, list(range(L)), ntiles, False)
        sh = dict(shared)
        for k in layer_keys:
            sh[k] = np.ascontiguousarray(shared[k][:L])
        in_maps = [dict(sh, **pc) for pc in percore]
        res = run_bass_kernel_spmd(nc, in_maps, core_ids=list(range(n_cores)))
        for i in range(n_cores):
            b, q = divmod(i, cores_per_seq)
            out[b, q * own:(q + 1) * own] = res.results[i]["yout"]
    else:
        nc = _get_prog(1, [0], ntiles, True)
        cur = [pc["xin"] for pc in percore]
        for l in range(L):
            sh = dict(shared)
            for k in layer_keys:
                sh[k] = np.ascontiguousarray(shared[k][l:l + 1])
            in_maps = [dict(sh, **dict(pc, xin=cur[i])) for i, pc in enumerate(percore)]
            res = run_bass_kernel_spmd(nc, in_maps, core_ids=list(range(n_cores)))
            cur = [np.ascontiguousarray(res.results[i]["yout"]) for i in range(n_cores)]
        for i in range(n_cores):
            b, q = divmod(i, cores_per_seq)
            out[b, q * own:(q + 1) * own] = cur[i][512:]
    return out


FUSED = True


def kernel(**inputs):
    return run_model(inputs, fused=FUSED)
```

```python
import numpy as np
import concourse.bass as bass
import concourse.mybir as mybir
from concourse.bass_utils import run_bass_kernel_spmd

F32 = mybir.dt.float32
BF16 = mybir.dt.bfloat16
AF = mybir.ActivationFunctionType
ALU = mybir.AluOpType

D = 1024
DFF = 2816
NJ = 22
DEPTH = 4
ALPHA = (2.0 * DEPTH) ** 0.25
EPS = 1e-5
WINDOWS = (2, 4, 8, 16)
NSLOT = 7
NSC = 22
SAME_ENGINE_SYNC = True


class Buf:
    __slots__ = ("name", "w", "r", "dsem", "dcnt", "ssem", "scnt")

    def __init__(self, name):
        self.name = name
        self.w = None
        self.r = {}
        self.dsem = None
        self.dcnt = 0
        self.ssem = None
        self.scnt = 0


class Sync:
    def __init__(self, nc, same=True):
        self.nc = nc
        self.eng = {"pe": nc.tensor, "act": nc.scalar, "dve": nc.vector, "pool": nc.gpsimd, "sp": nc.sync}
        self.esem = {k: nc.alloc_semaphore("es_" + k) for k in self.eng}
        self.ecnt = {k: 0 for k in self.eng}
        self.waited = {k: {} for k in self.eng}
        self.same = same
        self.sems = {}

    def _wait(self, e, tok):
        sem, val = tok
        if sem is self.esem[e] and (e == "pe" or not self.same):
            return
        key = id(sem)
        if self.waited[e].get(key, 0) >= val:
            return
        self.eng[e].wait_ge(sem, val)
        self.waited[e][key] = val

    def _deps(self, e, reads, writes):
        for b in reads:
            if b.w is not None:
                self._wait(e, b.w)
        for b in writes:
            if b.w is not None:
                self._wait(e, b.w)
            for t in b.r.values():
                self._wait(e, t)

    def _mark(self, tok, reads, writes):
        k = id(tok[0])
        for b in reads:
            b.r[k] = tok
        for b in writes:
            b.w = tok
            b.r = {}

    def group(self, e, fns, reads=(), writes=()):
        self._deps(e, reads, writes)
        ins = None
        for fn in fns:
            ins = fn(self.eng[e])
        self.ecnt[e] += 1
        ins.then_inc(self.esem[e], 1)
        tok = (self.esem[e], self.ecnt[e])
        self._mark(tok, reads, writes)
        return tok

    def op(self, e, fn, reads=(), writes=()):
        return self.group(e, [fn], reads, writes)

    def _dsem_inc(self, q, owner, ins):
        if q == "pool":
            if owner.ssem is None:
                owner.ssem = self.nc.alloc_semaphore("ss_" + owner.name)
            owner.scnt += 16
            ins.then_inc(owner.ssem, 16)
            return (owner.ssem, owner.scnt)
        if owner.dsem is None:
            owner.dsem = self.nc.alloc_semaphore("ds_" + owner.name)
        owner.dcnt += 16
        ins.then_inc(owner.dsem, 16)
        return (owner.dsem, owner.dcnt)

    def dma(self, q, out, in_, reads=(), writes=(), owner=None, **kw):
        self._deps(q, reads, writes)
        if owner is None:
            owner = (list(writes) + list(reads))[0]
        ins = self.eng[q].dma_start(out=out, in_=in_, **kw)
        tok = self._dsem_inc(q, owner, ins)
        self._mark(tok, reads, writes)
        return tok

    def dma2(self, q, pairs, reads=(), writes=()):
        self._deps(q, reads, writes)
        owner = (list(writes) + list(reads))[0]
        for out, in_ in pairs:
            ins = self.eng[q].dma_start(out=out, in_=in_)
            tok = self._dsem_inc(q, owner, ins)
        self._mark(tok, reads, writes)
        return tok

    def wait_bufs(self, e, bufs):
        for b in bufs:
            if b.w is not None:
                self._wait(e, b.w)
            for t in b.r.values():
                self._wait(e, t)

    def barrier(self, bufs):
        for e in self.eng:
            for o in self.eng:
                if o != e and self.ecnt[o] > 0:
                    self._wait(e, (self.esem[o], self.ecnt[o]))
            self.wait_bufs(e, bufs)


def build(nl, first_chunks, ntiles, out_all):
    NCH = ntiles * 4
    NTOK = NCH * 128
    nc = bass.Bass("TRN2", target_bir_lowering=False)
    S = Sync(nc, SAME_ENGINE_SYNC)
    allbufs = []

    def mkbuf(name):
        b = Buf(name)
        allbufs.append(b)
        return b

    def din(name, shape):
        return nc.dram_tensor(name, list(shape), F32, kind="ExternalInput").ap()

    xin = din("xin", [NTOK, D])
    cT_d = din("cT", [128, 8])
    mcol_d = din("mcol", [128, 1])
    Pprev_d = din("Pprev", [128, 4, 128])
    Pcur_d = din("Pcur", [128, 4, 128])
    Pfirst_d = din("Pfirst", [128, 4, 128])
    maskT_d = din("maskT", [128, 128])
    ident_d = din("ident", [128, 128])
    ada_w = din("ada_w", [nl, D, 6 * D])
    vecsT_d = din("vecsT", [nl, 128, 160])
    w_in = din("w_in", [nl, D, 1536])
    pool_w = din("pool_w", [nl, 4, 128, 128])
    sgu_wT = din("sgu_wT", [nl, 4, 128, 128])
    sgu_ln_b = din("sgu_ln_b", [nl, 512])
    sgu_b = din("sgu_b", [nl, 512])
    w_out = din("w_out", [nl, D, D])
    ln1_g = din("ln1_g", [nl, D])
    ln1_b = din("ln1_b", [nl, D])
    w_up = din("w_up", [nl, D, 2 * DFF])
    w_down = din("w_down", [nl, DFF, D])
    ln2_g = din("ln2_g", [nl, D])
    ln2_b = din("ln2_b", [nl, D])
    n_out_ch = NCH if out_all else NCH - 4
    yout = nc.dram_tensor("yout", [n_out_ch * 128, D], F32, kind="ExternalOutput").ap()
    wsc = nc.dram_tensor("wsc", [nl, NSC, 128, 4096], BF16).ap()
    gsc = nc.dram_tensor("gsc", [nl, 2, D], F32).ap()
    b1sc = nc.dram_tensor("b1sc", [nl, 2 * D], BF16).ap()
    Bb1sc = [mkbuf("b1sc%d" % l) for l in range(nl)]
    Bwsc = [[mkbuf("wsc%d_%d" % (l, s)) for s in range(NSC)] for l in range(nl)]
    Bgsc = [[mkbuf("gsc%d_%d" % (l, s)) for s in range(2)] for l in range(nl)]
    Byout = mkbuf("yout")

    def sb(name, shape, dt=F32):
        t = nc.alloc_sbuf_tensor("s_" + name, list(shape), dt)
        return t, mkbuf(name)

    ident_bf, Bident = sb("ident_bf", [128, 128], BF16)
    ident_f, Bidentf = sb("ident_f", [128, 128])
    ones_f, Bonesf = sb("ones_f", [128, 128])
    ones_bf, Bonesbf = sb("ones_bf", [128, 128], BF16)
    Pprev, BPprev = sb("Pprev", [128, 4, 128], BF16)
    Pcur, BPcur = sb("Pcur", [128, 4, 128], BF16)
    Pfirst, BPfirst = sb("Pfirst", [128, 4, 128], BF16)
    mcol, Bmcol = sb("mcol", [128, 1])
    cact, Bcact = sb("cact", [128, 8], BF16)
    mhalf, Bmhalf = sb("mhalf", [128, 1])
    vt = [sb("vt%d" % l, [128, 160]) for l in range(nl)]
    bT2 = [sb("bT2_%d" % l, [128, 8]) for l in range(nl)]
    modT = [sb("modT%d" % l, [128, 48]) for l in range(nl)]
    gT = [sb("gT%d" % l, [128, 16]) for l in range(nl)]
    rall = [sb("rall%d" % l, [128, 48]) for l in range(nl)]
    cbp = [sb("cbp%d" % l, [128, NJ]) for l in range(nl)]
    Cc = [sb("Cc%d" % l, [128, 4, 128]) for l in range(nl)]
    rvrow = [sb("rvrow%d" % l, [2, 512], BF16) for l in range(nl)]
    poolw = [sb("poolw%d" % l, [128, 4, 128], BF16) for l in range(nl)]
    WsT = [sb("WsT%d" % l, [128, 4, 128], BF16) for l in range(nl)]
    aprev = [sb("aprev%d" % l, [128, 512], BF16) for l in range(nl)]
    gcarry = [sb("gcarry%d" % l, [128, NJ, 2]) for l in range(nl)]

    ring = [sb("ring%d" % i, [128, 8, 512], BF16) for i in range(NSLOT)]
    ring_cnt = [0]

    def ring_next():
        i = ring_cnt[0] % NSLOT
        ring_cnt[0] += 1
        return ring[i]

    def ring_load(l, s, nk=8):
        t, B = ring_next()
        S.dma("sp", t[:, 0:nk, :], wsc[l, s, :, 0:nk * 512].rearrange("p (k n) -> p k n", n=512),
              reads=[Bwsc[l][s]], writes=[B])
        return t, B

    pbank = [(nc.alloc_psum_tensor("pb%d" % i, [128, 512], F32), mkbuf("pb%d" % i)) for i in range(6)]
    ptp = [(nc.alloc_psum_tensor("tp%d" % i, [128, 8, 128], BF16), mkbuf("tp%d" % i)) for i in range(2)]
    pcnt = [0, 0]

    def next_bank():
        i = pcnt[0] % 6
        pcnt[0] += 1
        return pbank[i]

    def next_tp():
        i = pcnt[1] % 2
        pcnt[1] += 1
        return ptp[i]

    y, _ = sb("y", [128, 4, D])
    By = [mkbuf("y%d" % c) for c in range(4)]
    yb, _ = sb("yb", [128, 4, D], BF16)
    Byb = [mkbuf("yb%d" % c) for c in range(4)]
    hT, _ = sb("hT", [128, 8, 512], BF16)
    BhT = [mkbuf("hT%d" % i) for i in range(4)]
    hid, _ = sb("hid", [128, NJ, 512], BF16)
    Bhid = [mkbuf("hid%d" % j) for j in range(NJ)]
    lnt = [[sb("lnt%d%d" % (s, i), [128, D]) for i in range(2)] for s in range(2)]
    acur, _ = sb("acur", [128, 4, 512], BF16)
    Bacur = [mkbuf("acur%d" % c) for c in range(4)]
    gv = [sb("gv%d" % i, [128, 512]) for i in range(2)]
    vhat, _ = sb("vhat", [128, 4, 512], BF16)
    Bvhat = [mkbuf("vhat%d" % c) for c in range(4)]
    gu = [sb("gu%d" % i, [128, 512]) for i in range(2)]
    zz = [sb("zz%d" % i, [128, 512]) for i in range(2)]
    pooledT, _ = sb("pooledT", [128, 4, 512], BF16)
    BpooledT = [mkbuf("pooledT%d" % g) for g in range(4)]
    mixT, _ = sb("mixT", [128, 8, 512], BF16)
    BmixT = [mkbuf("mixT%d" % e) for e in range(8)]
    c1 = [sb("c1_%d" % i, [128, 512]) for i in range(3)]
    gel = [sb("gel%d" % i, [128, 512]) for i in range(2)]
    stt_, _ = sb("stats", [128, 8, 12])
    Bst = [mkbuf("st%d" % i) for i in range(8)]
    mv_, _ = sb("mv", [128, 8, 4])
    Bmv = [mkbuf("mv%d" % i) for i in range(8)]
    stcnt = [0]
    corr, Bcorr = sb("corr", [128, NJ, 2])
    tmp22 = [sb("tmp22_%d" % i, [128, NJ]) for i in range(2)]
    rot = {"gv": 0, "gu": 0, "zz": 0, "c1": 0, "gel": 0}

    def nxt(name, lst):
        i = rot[name] % len(lst)
        rot[name] += 1
        return lst[i]

    stage = [(hid[:].rearrange("p j n -> p (j n)")[:, 0:8192].bitcast(F32).rearrange("p (k n) -> p k n", n=512), mkbuf("stage0")),
             (y[:].rearrange("p c (a n) -> p (c a) n", n=512), mkbuf("stage1"))]
    gbc = [lnt[0][0], lnt[0][1]]
    rowsA, BrowsA = gv[0][0][0:1, :], mkbuf("rowsA")
    rowsB, BrowsB = gv[1][0][0:1, :], mkbuf("rowsB")
    rowsC, BrowsC = gu[0][0][0:1, :], mkbuf("rowsC")
    wsf, Bwsf = gu[1][0][:].rearrange("p (h t) -> p h t", t=128), mkbuf("wsf")
    g8, Bg8 = zz[0][0][0:8, 0:128], mkbuf("g8")

    S.dma("pool", ident_bf[:], ident_d, writes=[Bident])
    S.dma("sp", ident_f[:], ident_d, writes=[Bidentf])
    S.dma("pool", Pprev[:], Pprev_d, writes=[BPprev])
    S.dma("pool", Pcur[:], Pcur_d, writes=[BPcur])
    S.dma("pool", Pfirst[:], Pfirst_d, writes=[BPfirst])
    S.dma("sp", mcol[:], mcol_d, writes=[Bmcol])
    S.op("dve", lambda e: e.memset(ones_f[:], 1.0), writes=[Bonesf])
    S.op("dve", lambda e: e.memset(ones_bf[:], 1.0), writes=[Bonesbf])
    S.op("dve", lambda e: e.memset(mhalf[:], -0.5), writes=[Bmhalf])
    for l in range(nl):
        S.op("dve", lambda e, l=l: e.memset(aprev[l][0][:], 0.0), writes=[aprev[l][1]])
        S.op("dve", lambda e, l=l: e.memset(gcarry[l][0][:], 0.0), writes=[gcarry[l][1]])
    ctmp, Bctmp = sb("ctmp", [128, 8])
    S.dma("sp", ctmp[:], cT_d, writes=[Bctmp])
    S.op("act", lambda e: e.activation(out=cact[:], in_=ctmp[:], func=AF.Silu), reads=[Bctmp], writes=[Bcact])

    def bc_last(ap2, n):
        return ap2.unsqueeze(2).to_broadcast([128, ap2.shape[1], n])

    def bc_mid(ap2, k):
        return ap2.unsqueeze(1).to_broadcast([128, k, ap2.shape[1]])

    def prepass_a(l):
        vtl, Bvt = vt[l]
        S.dma("sp", vtl[:], vecsT_d[l], writes=[Bvt])
        psm, Bpsm = next_bank()
        for i in range(12):
            t, B = ring_next()
            S.dma("pool", t[:], ada_w[l, :, i * 512:(i + 1) * 512].rearrange("(k p) n -> p k n", p=128), writes=[B])
            fns = []
            for j in range(4):
                for k in range(8):
                    fns.append(lambda e, t=t, j=j, k=k, i=i: e.matmul(
                        psm[:, i * 4 + j:i * 4 + j + 1], lhsT=t[:, k, j * 128:(j + 1) * 128], rhs=cact[:, k:k + 1],
                        start=(k == 0), stop=(k == 7)))
            S.group("pe", fns, reads=[B, Bcact], writes=[Bpsm])
        mt, Bmt = modT[l]
        S.op("dve", lambda e: e.tensor_tensor(out=mt[:], in0=psm[:, 0:48], in1=vtl[:, 96:144], op=ALU.add),
             reads=[Bpsm, Bvt], writes=[Bmt])
        g, Bg = gT[l]
        S.op("dve", lambda e: e.tensor_scalar_add(out=g[:, 0:8], in0=mt[:, 8:16], scalar1=1.0), reads=[Bmt], writes=[Bg])
        S.op("dve", lambda e: e.tensor_scalar_add(out=g[:, 8:16], in0=mt[:, 32:40], scalar1=1.0), reads=[Bmt], writes=[Bg])
        b2, Bb2 = bT2[l]
        S.op("dve", lambda e: e.tensor_tensor(out=b2[:], in0=vtl[:, 152:160], in1=g[:, 8:16], op=ALU.mult), reads=[Bvt, Bg], writes=[Bb2])
        S.op("dve", lambda e: e.tensor_tensor(out=b2[:], in0=b2[:], in1=mt[:, 24:32], op=ALU.add), reads=[Bmt, Bb2], writes=[Bb2])
        S.op("dve", lambda e: e.tensor_tensor(out=g[:, 8:16], in0=g[:, 8:16], in1=vtl[:, 144:152], op=ALU.mult), reads=[Bvt, Bg], writes=[Bg])
        stg = lnt[1][0]
        S.dma("sp", stg[0][0:1, :], ln1_b[l:l + 1, :], writes=[stg[1]])
        S.op("dve", lambda e: e.tensor_copy(out=yb[0:1, 0, :], in_=stg[0][0:1, :]), reads=[stg[1]], writes=[Byb[0]])
        S.op("dve", lambda e: e.tensor_tensor(out=yb[0:1, 1, :], in0=stg[0][0:1, :], in1=yb[0:1, 0, :], op=ALU.subtract),
             reads=[stg[1], Byb[0]], writes=[Byb[1]])
        S.dma("sp", b1sc[l:l + 1, :].rearrange("o (a n) -> o a n", a=2), yb[0:1, 0:2, :], reads=[Byb[0], Byb[1]], writes=[Bb1sc[l]], owner=Byb[0])

    def prepass_w(l):
        vtl, Bvt = vt[l]
        mt, Bmt = modT[l]
        g, Bg = gT[l]
        b2, Bb2 = bT2[l]
        for s in range(2):
            pt, Bpt = next_bank()
            col = 16 + 24 * s
            S.op("pe", lambda e, pt=pt, col=col: e.transpose(pt[0:8, 0:128], mt[:, col:col + 8], ident_f[:]),
                 reads=[Bmt, Bidentf], writes=[Bpt])
            S.op("dve", lambda e, pt=pt: e.tensor_copy(out=g8[:], in_=pt[0:8, 0:128]), reads=[Bpt], writes=[Bg8])
            S.dma("sp", gsc[l, s].rearrange("(k p) -> k p", p=128), g8[:], reads=[Bg8], writes=[Bgsc[l][s]], owner=Bg8)
            S.dma("sp", gbc[s][0][:], gsc[l, s].partition_broadcast(128), reads=[Bgsc[l][s]], writes=[gbc[s][1]])
            if s == 1:
                S.op("dve", lambda e: e.tensor_scalar_mul(out=gbc[1][0][:], in0=gbc[1][0][:], scalar1=1.0 / ALPHA),
                     reads=[gbc[1][1]], writes=[gbc[1][1]])
        psr, Bpsr = next_bank()
        sidx = [0]

        def get_stage():
            i = sidx[0] % 2
            sidx[0] += 1
            return stage[i]

        def emit_scaled(st, Bs, s, nk, scale_ap, Bscale, eng, rowcols=None):
            t, B = ring_next()
            if eng == "act":
                fns = [lambda e, k=k: e.activation(out=t[:, k, :], in_=st[:, k, :], func=AF.Identity, scale=g[:, rowcols + k:rowcols + k + 1])
                       for k in range(nk)]
                S.group("act", fns, reads=[Bs, Bscale], writes=[B])
            else:
                S.op("dve", lambda e: e.tensor_tensor(out=t[:, 0:nk, :], in0=st[:, 0:nk, :], in1=scale_ap, op=ALU.mult),
                     reads=[Bs, Bscale], writes=[B])
            S.dma("act" if eng != "act" else "sp", wsc[l, s, :, 0:nk * 512].rearrange("p (k n) -> p k n", n=512), t[:, 0:nk, :],
                  reads=[B], writes=[Bwsc[l][s]], owner=B)

        def fm_bias(st, Bs, off, shcol, outcol):
            src, Bsrc, c0 = (mt, Bmt, shcol) if shcol == 0 else (b2, Bb2, 0)
            fns = []
            for k in range(8):
                fns.append(lambda e, k=k: e.matmul(psr[:, outcol:outcol + 1], lhsT=st[:, k, off:off + 128],
                                                   rhs=src[:, c0 + k:c0 + k + 1], start=(k == 0), stop=(k == 7)))
            S.group("pe", fns, reads=[Bs, Bsrc], writes=[Bpsr])

        engs = ["dve", "act"]
        for s, c0 in ((0, 0), (1, 1024), (2, 512)):
            st, Bs = get_stage()
            S.dma("sp", st[:], w_in[l, :, c0:c0 + 512].rearrange("(k p) n -> p k n", p=128), writes=[Bs])
            if s == 1:
                pr, Bpr = next_bank()
                fns = [lambda e, k=k: e.matmul(pr[0:1, :], lhsT=mt[:, k:k + 1], rhs=st[:, k, :],
                                               start=(k == 0), stop=(k == 7)) for k in range(8)]
                S.group("pe", fns, reads=[Bs, Bmt], writes=[Bpr])
                rv, Brv = rvrow[l]
                S.op("act", lambda e: e.copy(out=rv[0:1, :], in_=pr[0:1, :]), reads=[Bpr], writes=[Brv])
                S.op("dve", lambda e: e.tensor_tensor(out=yb[0:1, 2, 0:512], in0=pr[0:1, :], in1=rv[0:1, :],
                                                      op=ALU.subtract), reads=[Bpr, Brv], writes=[Byb[2]])
                S.dma("sp", rv[1:2, :], yb[0:1, 2, 0:512], reads=[Byb[2]], writes=[Brv])
            if s == 2:
                for j in range(4):
                    fm_bias(st, Bs, j * 128, 0, 44 + j)
            emit_scaled(st, Bs, s, 8, bc_last(g[:, 0:8], 512), Bg, engs[s % 2], 0)
        for hf in range(2):
            st, Bs = get_stage()
            S.dma("sp", st[:], w_out[l, :, hf * 512:(hf + 1) * 512].rearrange("(k p) n -> p k n", p=128), writes=[Bs])
            emit_scaled(st, Bs, 3 + hf, 8, bc_mid(gbc[0][0][:, hf * 512:(hf + 1) * 512], 8), gbc[0][1], "dve")
        for i in range(11):
            j0 = 2 * i
            st, Bs = get_stage()
            S.dma2("sp", [(st[:, :, 0:256], w_up[l, :, j0 * 128:j0 * 128 + 256].rearrange("(k p) n -> p k n", p=128)),
                          (st[:, :, 256:512], w_up[l, :, DFF + j0 * 128:DFF + j0 * 128 + 256].rearrange("(k p) n -> p k n", p=128))],
                   writes=[Bs])
            for fc in range(2):
                fm_bias(st, Bs, fc * 128, 24, j0 + fc)
                fm_bias(st, Bs, 256 + fc * 128, 24, 22 + j0 + fc)
            emit_scaled(st, Bs, 5 + i, 8, bc_last(g[:, 8:16], 512), Bg, engs[i % 2], 8)
        for hf in range(2):
            for part, (ja, jb) in enumerate(((0, 8), (8, 16), (16, 22))):
                nk = jb - ja
                st, Bs = get_stage()
                S.dma("sp", st[:, 0:nk, :], w_down[l, ja * 128:jb * 128, hf * 512:(hf + 1) * 512].rearrange("(k p) n -> p k n", p=128), writes=[Bs])
                emit_scaled(st, Bs, 16 + hf * 3 + part, nk, bc_mid(gbc[1][0][:, hf * 512:(hf + 1) * 512], nk), gbc[1][1], "dve")
        ra, Bra = rall[l]
        S.op("dve", lambda e: e.tensor_copy(out=ra[:], in_=psr[:, 0:48]), reads=[Bpsr], writes=[Bra])
        cb, Bcb = cbp[l]
        S.op("dve", lambda e: e.tensor_tensor(out=cb[:], in0=vtl[:, 8:30], in1=vtl[:, 30:52], op=ALU.add), reads=[Bvt], writes=[Bcb])
        S.op("dve", lambda e: e.tensor_tensor(out=cb[:], in0=cb[:], in1=vtl[:, 52:74], op=ALU.add), reads=[Bvt, Bcb], writes=[Bcb])
        S.op("dve", lambda e: e.tensor_tensor(out=cb[:], in0=cb[:], in1=ra[:, 0:22], op=ALU.mult), reads=[Bra, Bcb], writes=[Bcb])
        S.op("dve", lambda e: e.tensor_tensor(out=cb[:], in0=cb[:], in1=vtl[:, 74:96], op=ALU.add), reads=[Bvt, Bcb], writes=[Bcb])
        S.dma("sp", wsf[:], sgu_wT[l].rearrange("h s t -> s h t"), writes=[Bwsf])
        mk, Bmk = stage[0]
        S.dma("sp", mk[:, 0, 0:128], maskT_d, writes=[Bmk])
        S.op("dve", lambda e: e.tensor_tensor(out=wsf[:], in0=wsf[:], in1=bc_mid(mk[:, 0, 0:128], 4), op=ALU.mult),
             reads=[Bwsf, Bmk], writes=[Bwsf])
        S.op("dve", lambda e: e.tensor_copy(out=WsT[l][0][:], in_=wsf[:]), reads=[Bwsf], writes=[WsT[l][1]])
        prs, Bprs = next_bank()
        S.op("pe", lambda e: e.matmul(prs[0:1, :], lhsT=ones_f[:, 0:1], rhs=wsf[:].rearrange("p h t -> p (h t)"),
                                      start=True, stop=True), reads=[Bwsf, Bonesf], writes=[Bprs])
        S.op("dve", lambda e: e.tensor_copy(out=rowsA[:], in_=prs[0:1, :]), reads=[Bprs], writes=[BrowsA])
        S.dma("sp", rowsB[:], sgu_ln_b[l:l + 1, :], writes=[BrowsB])
        S.dma("sp", rowsC[:], sgu_b[l:l + 1, :], writes=[BrowsC])
        pc, Bpc = next_bank()
        fns = []
        for h in range(4):
            hs = slice(h * 128, (h + 1) * 128)
            fns.append(lambda e, hs=hs: e.matmul(pc[:, hs], lhsT=rowsB[0:1, hs], rhs=rowsA[0:1, hs], start=True, stop=False))
            fns.append(lambda e, hs=hs: e.matmul(pc[:, hs], lhsT=ones_f[0:1, 0:128], rhs=rowsC[0:1, hs], start=False, stop=True))
        S.group("pe", fns, reads=[BrowsA, BrowsB, BrowsC, Bonesf], writes=[Bpc])
        S.op("dve", lambda e: e.tensor_copy(out=Cc[l][0][:].rearrange("p h t -> p (h t)"), in_=pc[:]), reads=[Bpc], writes=[Cc[l][1]])
        S.dma("pool", poolw[l][0][:], pool_w[l].rearrange("g c d -> c g d"), writes=[poolw[l][1]])

    def transposes(c_lo, nch):
        for ci in range(nch):
            tp, Btp = next_tp()
            fns = [lambda e, tp=tp, k=k, ci=ci: e.transpose(tp[:, k, :], yb[:, c_lo + ci, k * 128:(k + 1) * 128], ident_bf[:])
                   for k in range(8)]
            S.group("pe", fns, reads=[Byb[c_lo + ci], Bident], writes=[Btp])
            if ci % 2 == 0:
                S.op("dve", lambda e, tp=tp, ci=ci: e.tensor_copy(out=hT[:, :, ci * 128:(ci + 1) * 128], in_=tp[:]),
                     reads=[Btp], writes=[BhT[ci]])
            else:
                S.op("act", lambda e, tp=tp, ci=ci: e.copy(out=hT[:, :, ci * 128:(ci + 1) * 128], in_=tp[:]),
                     reads=[Btp], writes=[BhT[ci]])

    def ln_stats(src_ap_list, Bsrc, eps=EPS):
        i = stcnt[0] % 8
        stcnt[0] += 1
        n = len(src_ap_list)
        for q, ap in enumerate(src_ap_list):
            S.op("dve", lambda e, ap=ap, q=q: e.bn_stats(out=stt_[:, i, q * 6:(q + 1) * 6], in_=ap), reads=[Bsrc], writes=[Bst[i]])
        S.op("dve", lambda e: e.bn_aggr(out=mv_[:, i, 0:2], in_=stt_[:, i, 0:6 * n]), reads=[Bst[i]], writes=[Bmv[i]])
        S.op("dve", lambda e: e.tensor_scalar_add(out=mv_[:, i, 1:2], in0=mv_[:, i, 1:2], scalar1=eps), reads=[Bmv[i]], writes=[Bmv[i]])
        S.op("pool", lambda e: e.tensor_tensor(out=mv_[:, i, 2:3], in0=mv_[:, i, 1:2], in1=mhalf[:], op=ALU.pow),
             reads=[Bmv[i], Bmhalf], writes=[Bmv[i]])
        S.op("dve", lambda e: e.scalar_tensor_tensor(out=mv_[:, i, 3:4], in0=mv_[:, i, 0:1], scalar=-1.0, in1=mv_[:, i, 2:3],
                                                     op0=ALU.mult, op1=ALU.mult), reads=[Bmv[i]], writes=[Bmv[i]])
        return mv_[:, i, 0:1], mv_[:, i, 2:3], mv_[:, i, 3:4], Bmv[i]

    def resid_ln_a(c, ps_halves):
        for hf, (ps, Bps) in enumerate(ps_halves):
            sl = slice(hf * 512, (hf + 1) * 512)
            S.op("dve", lambda e, ps=ps, sl=sl: e.scalar_tensor_tensor(out=y[:, c, sl], in0=y[:, c, sl], scalar=ALPHA, in1=ps[:],
                                                                     op0=ALU.mult, op1=ALU.add), reads=[Bps, By[c]], writes=[By[c]])
        return ln_stats([y[:, c, 0:512], y[:, c, 512:1024]], By[c])

    def resid_ln1_b(c, st):
        mean, rstd, nmr, Bm = st
        S.op("act", lambda e: e.activation(out=yb[:, c, :], in_=y[:, c, :], func=AF.Identity, scale=rstd, bias=nmr),
             reads=[By[c], Bm], writes=[Byb[c]])
        S.op("act", lambda e: e.activation(out=y[:, c, :], in_=y[:, c, :], func=AF.Identity, scale=rstd, bias=nmr),
             reads=[By[c], Bm], writes=[By[c]])

    def resid_ln2_a(c, ps_halves, g1t):
        S.op("dve", lambda e: e.tensor_tensor(out=y[:, c, :], in0=y[:, c, :], in1=g1t[0][:], op=ALU.mult),
             reads=[By[c], g1t[1]], writes=[By[c]])
        for hf, (ps, Bps) in enumerate(ps_halves):
            sl = slice(hf * 512, (hf + 1) * 512)
            S.op("dve", lambda e, ps=ps, sl=sl: e.tensor_tensor(out=y[:, c, sl], in0=y[:, c, sl], in1=ps[:], op=ALU.add),
                 reads=[Bps, By[c]], writes=[By[c]])
        return ln_stats([y[:, c, 0:512], y[:, c, 512:1024]], By[c], EPS / (ALPHA * ALPHA))

    def resid_ln_b(c, st, lng, lnb):
        mean, rstd, nmr, Bm = st
        S.op("dve", lambda e: e.scalar_tensor_tensor(out=y[:, c, :], in0=y[:, c, :], scalar=mean, in1=lng[0][:],
                                                     op0=ALU.subtract, op1=ALU.mult), reads=[By[c], Bm, lng[1]], writes=[By[c]])
        S.op("dve", lambda e: e.scalar_tensor_tensor(out=y[:, c, :], in0=y[:, c, :], scalar=rstd, in1=lnb[0][:],
                                                     op0=ALU.mult, op1=ALU.add), reads=[By[c], Bm, lnb[1]], writes=[By[c]])
        if not state["skip_yb"]:
            S.op("act", lambda e: e.copy(out=yb[:, c, :], in_=y[:, c, :]), reads=[By[c]], writes=[Byb[c]])

    def mixer(l, c_lo, nch, first_own):
        T = nch * 128
        vtl, Bvt = vt[l]
        ra, Bra = rall[l]
        ap_, Bap = aprev[l]
        if first_own:
            S.op("dve", lambda e: e.tensor_scalar_mul(out=ap_[:], in0=ap_[:], scalar1=mcol[:, 0:1]), reads=[Bap, Bmcol], writes=[Bap])
        S.dma("sp", lnt[0][0][0][:], ln1_g[l].partition_broadcast(128), writes=[lnt[0][0][1]])
        transposes(c_lo, nch)
        hTall = BhT
        sa, Bsa = ring_load(l, 0)
        for ci in range(nch):
            ps, Bps = next_bank()
            fns = [lambda e, ps=ps, k=k, ci=ci: e.matmul(ps[:], lhsT=hT[:, k, ci * 128:(ci + 1) * 128], rhs=sa[:, k, :],
                                                        start=(k == 0), stop=(k == 7)) for k in range(8)]
            S.group("pe", fns, reads=[BhT[ci], Bsa], writes=[Bps])
            S.op("act", lambda e, ps=ps, ci=ci: e.copy(out=acur[:, ci, :], in_=ps[:]), reads=[Bps], writes=[Bacur[ci]])
        sv, Bsv = ring_load(l, 1)
        rv, Brv = rvrow[l]
        for ci in range(nch):
            ps, Bps = next_bank()
            fns = [lambda e, ps=ps, k=k, ci=ci: e.matmul(ps[:], lhsT=hT[:, k, ci * 128:(ci + 1) * 128], rhs=sv[:, k, :],
                                                        start=(k == 0), stop=False) for k in range(8)]
            fns.append(lambda e, ps=ps: e.matmul(ps[:], lhsT=ones_bf[0:2, 0:128], rhs=rv[0:2, :], start=False, stop=True))
            S.group("pe", fns, reads=[BhT[ci], Bsv, Brv, Bonesbf], writes=[Bps])
            g_, Bg_ = nxt("gv", gv)
            S.op("act", lambda e, ps=ps, g_=g_: e.activation(out=g_[:], in_=ps[:], func=AF.Gelu), reads=[Bps], writes=[Bg_])
            mean, rstd, nmr, Bm = ln_stats([g_[:]], Bg_)
            S.op("pool", lambda e, g_=g_, ci=ci, rstd=rstd, nmr=nmr: e.tensor_scalar(
                out=vhat[:, ci, :], in0=g_[:], scalar1=rstd, scalar2=nmr, op0=ALU.mult, op1=ALU.add),
                reads=[Bg_, Bm], writes=[Bvhat[ci]])
        pw, Bpw = poolw[l]
        for g in range(4):
            ps, Bps = next_bank()
            fns = []
            gs = slice(g * 128, (g + 1) * 128)
            for ci in range(nch):
                prev_ap = ap_[:, gs] if ci == 0 else acur[:, ci - 1, gs]
                Pc = Pfirst if (first_own and ci == 0) else Pcur
                fns.append(lambda e, ps=ps, ci=ci, prev_ap=prev_ap: e.matmul(ps[:, ci * 128:(ci + 1) * 128], lhsT=prev_ap,
                                                                             rhs=Pprev[:, g, :], start=True, stop=False))
                fns.append(lambda e, ps=ps, ci=ci, Pc=Pc: e.matmul(ps[:, ci * 128:(ci + 1) * 128], lhsT=acur[:, ci, gs],
                                                                   rhs=Pc[:, g, :], start=False, stop=True))
            S.group("pe", fns, reads=[Bap, BPprev, BPcur, BPfirst] + Bacur[:nch], writes=[Bps])
            S.op("dve", lambda e, ps=ps, g=g: e.tensor_copy(out=pooledT[:, g, 0:T], in_=ps[:, 0:T]), reads=[Bps], writes=[BpooledT[g]])
        S.op("pool", lambda e: e.tensor_copy(out=ap_[:], in_=acur[:, nch - 1, :]), reads=[Bacur[nch - 1]], writes=[Bap])
        for g in range(4):
            ps, Bps = next_bank()
            S.op("pe", lambda e, ps=ps, g=g: e.matmul(ps[:, 0:T], lhsT=pw[:, g, :], rhs=pooledT[:, g, 0:T], start=True, stop=True),
                 reads=[Bpw, BpooledT[g]], writes=[Bps])
            S.op("act", lambda e, ps=ps, g=g: e.activation(out=mixT[:, g, 0:T], in_=ps[:, 0:T], func=AF.Identity, scale=vtl[:, g:g + 1]),
                 reads=[Bps, Bvt], writes=[BmixT[g]])
        su, Bsu = ring_load(l, 2)
        wst, Bwst = WsT[l]
        cc, Bcc = Cc[l]
        pzs, pus = [], []
        for h in range(4):
            hs = slice(h * 128, (h + 1) * 128)
            pz, Bpz = next_bank()
            fns = [lambda e, pz=pz, ci=ci: e.matmul(pz[:, ci * 128:(ci + 1) * 128], lhsT=vhat[:, ci, hs], rhs=wst[:, h, :],
                                                   start=True, stop=True) for ci in range(nch)]
            S.group("pe", fns, reads=Bvhat[:nch] + [Bwst], writes=[Bpz])
            pzs.append((pz, Bpz))
            if h < 2:
                zz_, Bzz = zz[h]
                S.op("dve", lambda e, pz=pz, zz_=zz_: e.scalar_tensor_tensor(
                    out=zz_[:, 0:T].rearrange("p (c t) -> p c t", t=128), in0=pz[:, 0:T].rearrange("p (c t) -> p c t", t=128),
                    scalar=vtl[:, 4 + h:5 + h], in1=cc[:, h, :].unsqueeze(1).to_broadcast([128, nch, 128]),
                    op0=ALU.mult, op1=ALU.add), reads=[Bpz, Bvt, Bcc], writes=[Bzz])
        for h in range(4):
            hs = slice(h * 128, (h + 1) * 128)
            ps, Bps = next_bank()
            fns = [lambda e, ps=ps, k=k: e.matmul(ps[:, 0:T], lhsT=su[:, k, hs], rhs=hT[:, k, 0:T], start=(k == 0), stop=(k == 7))
                   for k in range(8)]
            S.group("pe", fns, reads=hTall + [Bsu], writes=[Bps])
            pus.append((ps, Bps))
        for h in range(4):
            ps, Bps = pus[h]
            gu_, Bgu = nxt("gu", gu)
            S.op("act", lambda e, ps=ps, gu_=gu_: e.activation(out=gu_[:, 0:T], in_=ps[:, 0:T], func=AF.Gelu, bias=ra[:, 44 + h:45 + h]),
                 reads=[Bps, Bra], writes=[Bgu])
            zz_, Bzz = zz[h % 2]
            if h >= 2:
                pz, Bpz = pzs[h]
                S.op("dve", lambda e, pz=pz, zz_=zz_: e.scalar_tensor_tensor(
                    out=zz_[:, 0:T].rearrange("p (c t) -> p c t", t=128), in0=pz[:, 0:T].rearrange("p (c t) -> p c t", t=128),
                    scalar=vtl[:, 4 + h:5 + h], in1=cc[:, h, :].unsqueeze(1).to_broadcast([128, nch, 128]),
                    op0=ALU.mult, op1=ALU.add), reads=[Bpz, Bvt, Bcc], writes=[Bzz])
            S.op("dve", lambda e, zz_=zz_, gu_=gu_: e.tensor_tensor(out=mixT[:, 4 + h, 0:T], in0=gu_[:, 0:T], in1=zz_[:, 0:T], op=ALU.mult),
                 reads=[Bgu, Bzz], writes=[BmixT[4 + h]])
        so = [ring_load(l, 3), ring_load(l, 4)]
        if state["ycopy"]:
            for c in range(4):
                S.op("dve", lambda e, c=c: e.tensor_copy(out=y[:, c, :], in_=xpre[:, c, :]), reads=Bhid, writes=[By[c]])
            state["ycopy"] = False
        pend = None
        for ci in range(nch):
            halves = []
            for hf in range(2):
                ps, Bps = next_bank()
                w_, Bw_ = so[hf]
                fns = [lambda e, ps=ps, k=k, w_=w_: e.matmul(ps[:], lhsT=mixT[:, k, ci * 128:(ci + 1) * 128], rhs=w_[:, k, :],
                                                            start=(k == 0), stop=(k == 7)) for k in range(8)]
                S.group("pe", fns, reads=BmixT + [Bw_], writes=[Bps])
                halves.append((ps, Bps))
            if pend is not None and ci == nch - 1:
                resid_ln1_b(pend[0], pend[1])
                pend = None
            st = resid_ln_a(c_lo + ci, halves)
            if pend is not None:
                resid_ln1_b(pend[0], pend[1])
            pend = (c_lo + ci, st)
        resid_ln1_b(pend[0], pend[1])

    def ffn(l, c_lo, nch, first_own):
        T = nch * 128
        vtl, Bvt = vt[l]
        ra, Bra = rall[l]
        cb, Bcb = cbp[l]
        gc, Bgc = gcarry[l]
        if first_own:
            rgb = ra[:, 0:22].unsqueeze(2).to_broadcast([128, NJ, 2])
            S.op("dve", lambda e: e.tensor_tensor(out=gc[:], in0=gc[:], in1=rgb, op=ALU.add), reads=[Bgc, Bra], writes=[Bgc])
            S.op("dve", lambda e: e.tensor_scalar_mul(out=gc[:], in0=gc[:], scalar1=mcol[:, 0:1]), reads=[Bgc, Bmcol], writes=[Bgc])
            S.op("dve", lambda e: e.tensor_tensor(out=gc[:], in0=gc[:], in1=rgb, op=ALU.subtract), reads=[Bgc, Bra], writes=[Bgc])
        lng, lnb = lnt[1]
        rvt, Brvt = lnt[0][1][0][0:2, 0:512].bitcast(BF16), lnt[0][1][1]
        S.dma("sp", rvt, b1sc[l].rearrange("(a n) -> a n", a=2), reads=[Bb1sc[l]], writes=[Brvt])
        S.dma("sp", lng[0][:], ln2_g[l].partition_broadcast(128), writes=[lng[1]])
        S.dma("sp", lnb[0][:], ln2_b[l].partition_broadcast(128), writes=[lnb[1]])
        transposes(c_lo, nch)
        t0, Bt0 = tmp22[0]
        S.op("dve", lambda e: e.tensor_tensor(out=corr[:, :, 1], in0=gc[:, :, 1], in1=vtl[:, 8:30], op=ALU.mult), reads=[Bgc, Bvt], writes=[Bcorr])
        S.op("dve", lambda e: e.tensor_tensor(out=t0[:], in0=gc[:, :, 1], in1=vtl[:, 30:52], op=ALU.mult), reads=[Bgc, Bvt], writes=[Bt0])
        S.op("dve", lambda e: e.tensor_tensor(out=corr[:, :, 0], in0=gc[:, :, 0], in1=vtl[:, 8:30], op=ALU.mult), reads=[Bgc, Bvt], writes=[Bcorr])
        S.op("dve", lambda e: e.tensor_tensor(out=corr[:, :, 0], in0=corr[:, :, 0], in1=t0[:], op=ALU.add), reads=[Bt0, Bcorr], writes=[Bcorr])
        for i in range(11):
            su, Bsu = ring_load(l, 5 + i)
            for fc in range(2):
                j = 2 * i + fc
                pg, Bpg = next_bank()
                pv, Bpv = next_bank()
                fns = [lambda e, pg=pg, k=k: e.matmul(pg[:, 0:T], lhsT=su[:, k, fc * 128:(fc + 1) * 128], rhs=hT[:, k, 0:T],
                                                     start=(k == 0), stop=(k == 7)) for k in range(8)]
                S.group("pe", fns, reads=BhT + [Bsu], writes=[Bpg])
                fns = [lambda e, pv=pv, k=k: e.matmul(pv[:, 0:T], lhsT=su[:, k, 256 + fc * 128:256 + (fc + 1) * 128], rhs=hT[:, k, 0:T],
                                                     start=(k == 0), stop=(k == 7)) for k in range(8)]
                S.group("pe", fns, reads=BhT + [Bsu], writes=[Bpv])
                c_, Bc_ = nxt("c1", c1)
                S.op("act", lambda e, pg=pg, c_=c_: e.activation(out=c_[:, 0:T], in_=pg[:, 0:T], func=AF.Identity,
                                                                scale=vtl[:, 52 + j:53 + j], bias=cb[:, j:j + 1]),
                     reads=[Bpg, Bvt, Bcb], writes=[Bc_])
                S.op("dve", lambda e, c_=c_: e.tensor_tensor(out=c_[:, 0:2], in0=c_[:, 0:2], in1=corr[:, j, :], op=ALU.add),
                     reads=[Bc_, Bcorr], writes=[Bc_])
                S.op("dve", lambda e, pg=pg, c_=c_: e.scalar_tensor_tensor(out=c_[:, 1:T], in0=pg[:, 0:T - 1], scalar=vtl[:, 30 + j:31 + j],
                                                                          in1=c_[:, 1:T], op0=ALU.mult, op1=ALU.add),
                     reads=[Bpg, Bc_, Bvt], writes=[Bc_])
                S.op("dve", lambda e, pg=pg, c_=c_: e.scalar_tensor_tensor(out=c_[:, 2:T], in0=pg[:, 0:T - 2], scalar=vtl[:, 8 + j:9 + j],
                                                                          in1=c_[:, 2:T], op0=ALU.mult, op1=ALU.add),
                     reads=[Bpg, Bc_, Bvt], writes=[Bc_])
                S.op("act", lambda e, pg=pg: e.copy(out=gc[:, j, :], in_=pg[:, T - 2:T]), reads=[Bpg, Bcorr], writes=[Bgc])
                ge, Bge = nxt("gel", gel)
                S.op("act", lambda e, c_=c_, ge=ge: e.activation(out=ge[:, 0:T], in_=c_[:, 0:T], func=AF.Gelu), reads=[Bc_], writes=[Bge])
                S.op("dve", lambda e, pv=pv, ge=ge: e.scalar_tensor_tensor(out=hid[:, j, 0:T], in0=pv[:, 0:T], scalar=ra[:, 22 + j:23 + j],
                                                                          in1=ge[:, 0:T], op0=ALU.add, op1=ALU.mult),
                     reads=[Bpv, Bge, Bra], writes=[Bhid[j]])
        sd = [[ring_load(l, 16 + hf * 3 + p, nk) for p, nk in enumerate((8, 8, 6))] for hf in range(2)]
        pend = None
        for ci in range(nch):
            halves = []
            for hf in range(2):
                ps, Bps = next_bank()
                fns = []
                for j in range(NJ):
                    w_ = sd[hf][j // 8][0]
                    fns.append(lambda e, ps=ps, j=j, w_=w_: e.matmul(ps[:], lhsT=hid[:, j, ci * 128:(ci + 1) * 128], rhs=w_[:, j % 8, :],
                                                                    start=(j == 0), stop=(j == NJ - 1)))
                fns[-1] = (lambda e, ps=ps, w_=sd[hf][2][0]: e.matmul(ps[:], lhsT=hid[:, NJ - 1, ci * 128:(ci + 1) * 128], rhs=w_[:, (NJ - 1) % 8, :],
                                                                     start=False, stop=False))
                cs = slice(hf * 512, (hf + 1) * 512)
                fns.append(lambda e, ps=ps, cs=cs: e.matmul(ps[:], lhsT=ones_bf[0:2, 0:128], rhs=rvt[0:2, cs], start=False, stop=True))
                S.group("pe", fns[0:16], reads=Bhid[0:16] + [b for _, b in sd[hf]], writes=[Bps])
                S.group("pe", fns[16:], reads=Bhid[16:] + [b for _, b in sd[hf]] + [Brvt, Bonesbf], writes=[Bps])
                halves.append((ps, Bps))
            if pend is not None and ci == nch - 1:
                resid_ln_b(pend[0], pend[1], lng, lnb)
                pend = None
            if ci == nch - 1 and state["prefetch"] is not None:
                tn = state["prefetch"]
                S.dma("sp", xpre, xin[tn * 512:(tn + 1) * 512, :].rearrange("(c p) d -> p c d", p=128), writes=Bhid)
                state["prefetch"] = None
            st = resid_ln2_a(c_lo + ci, halves, lnt[0][0])
            if pend is not None:
                resid_ln_b(pend[0], pend[1], lng, lnb)
            pend = (c_lo + ci, st)
        resid_ln_b(pend[0], pend[1], lng, lnb)

    state = {"skip_yb": False, "ycopy": False, "prefetch": None}
    xpre = hid[:].rearrange("p j n -> p (j n)")[:, 0:8192].bitcast(F32).rearrange("p (c n) -> p c n", n=D)

    prepass_a(0)
    for l in range(nl):
        if l + 1 < nl:
            prepass_a(l + 1)
        prepass_w(l)
    S.barrier(allbufs)

    for t in range(ntiles):
        if t == 0:
            for c in range(4):
                S.dma("pool", y[:, c, :], xin[(t * 4 + c) * 128:(t * 4 + c + 1) * 128, :], writes=[By[c]])
                S.op("act", lambda e, c=c: e.copy(out=yb[:, c, :], in_=y[:, c, :]), reads=[By[c]], writes=[Byb[c]])
        else:
            for c in range(4):
                S.op("act", lambda e, c=c: e.copy(out=yb[:, c, :], in_=xpre[:, c, :]), reads=Bhid, writes=[Byb[c]])
            state["ycopy"] = True
        for l in range(nl):
            c_lo = first_chunks[l] if t == 0 else 0
            if t == 0 and c_lo >= 4:
                continue
            nch = 4 - c_lo
            last = (l == nl - 1)
            mixer(l, c_lo, nch, t == 1)
            if last and t + 1 < ntiles:
                state["prefetch"] = t + 1
            state["skip_yb"] = last
            ffn(l, c_lo, nch, t == 1)
            state["skip_yb"] = False
        for c in range(4):
            if out_all:
                row = (t * 4 + c) * 128
            elif t == 0:
                continue
            else:
                row = ((t - 1) * 4 + c) * 128
            S.dma("pool", yout[row:row + 128, :], y[:, c, :], reads=[By[c]], owner=By[c])
    S.barrier(allbufs)
    return nc


def _pool_mats():
    Pprev = np.zeros((4, 128, 128), np.float32)
    Pcur = np.zeros((4, 128, 128), np.float32)
    Pfirst = np.zeros((4, 128, 128), np.float32)
    t = np.arange(128)[None, :]
    s = np.arange(128)[:, None]
    for g, w in enumerate(WINDOWS):
        Pcur[g] = np.where((s <= t) & (s >= t - w + 1), 1.0 / w, 0.0) - np.where(s == t, 1.0, 0.0)
        Pprev[g] = np.where((s - 128) >= t - w + 1, 1.0 / w, 0.0)
        cnt = np.minimum(t + 1, w).astype(np.float32)
        Pfirst[g] = np.where((s <= t) & (s >= t - w + 1), 1.0 / cnt, 0.0) - np.where(s == t, 1.0, 0.0)
    tr = lambda a: np.ascontiguousarray(a.transpose(1, 0, 2))
    return tr(Pprev), tr(Pcur), tr(Pfirst)


_PROG = {}


def _get_prog(nl, first_chunks, ntiles, out_all):
    key = (nl, tuple(first_chunks), ntiles, out_all)
    if key not in _PROG:
        _PROG[key] = build(nl, first_chunks, ntiles, out_all)
    return _PROG[key]


def run_model(inputs, fused=True, cores_per_seq=4, n_cores=8, nlayers=DEPTH):
    x = np.asarray(inputs["x"], np.float32)
    c = np.asarray(inputs["c"], np.float32)
    B, SEQ, _ = x.shape
    own = SEQ // cores_per_seq
    ntiles = own // 512 + 1
    f32 = lambda k: np.ascontiguousarray(np.asarray(inputs[k], np.float32))
    Pprev, Pcur, Pfirst = _pool_mats()
    s_ = np.arange(128)[:, None]
    t_ = np.arange(128)[None, :]
    maskT = (s_ <= t_).astype(np.float32)
    ident = np.eye(128, dtype=np.float32)
    L = nlayers
    vecsT = np.zeros((L, 128, 160), np.float32)
    for l in range(L):
        cols = np.concatenate([
            f32("pool_scale")[l].reshape(4, 128), f32("sgu_ln_g")[l].reshape(4, 128),
            f32("conv_w")[l, 0].reshape(NJ, 128), f32("conv_w")[l, 1].reshape(NJ, 128), f32("conv_w")[l, 2].reshape(NJ, 128),
            f32("conv_b")[l].reshape(NJ, 128), f32("ada_b")[l].reshape(48, 128),
            f32("ln1_g")[l].reshape(8, 128), f32("ln1_b")[l].reshape(8, 128)], axis=0)
        vecsT[l] = cols.T
    sgu_wT = np.ascontiguousarray(f32("sgu_w").transpose(0, 1, 3, 2))
    shared = {
        "Pprev": Pprev, "Pcur": Pcur, "maskT": maskT, "ident": ident,
        "ada_w": f32("ada_w"), "vecsT": vecsT, "w_in": f32("w_in"), "pool_w": f32("pool_w"), "sgu_wT": sgu_wT,
        "sgu_ln_b": f32("sgu_ln_b"), "sgu_b": f32("sgu_b").reshape(-1, 512), "w_out": f32("w_out"),
        "ln1_g": f32("ln1_g"), "ln1_b": f32("ln1_b"), "w_up": f32("w_up"), "w_down": f32("w_down"),
        "ln2_g": f32("ln2_g"), "ln2_b": f32("ln2_b"),
    }
    layer_keys = ["ada_w", "vecsT", "w_in", "pool_w", "sgu_wT", "sgu_ln_b", "sgu_b", "w_out", "ln1_g", "ln1_b",
                  "w_up", "w_down", "ln2_g", "ln2_b"]
    percore = []
    for i in range(n_cores):
        b, q = divmod(i, cores_per_seq)
        if q == 0:
            halo = np.zeros((512, D), np.float32)
        else:
            halo = x[b, q * own - 512:q * own]
        xi = np.concatenate([halo, x[b, q * own:(q + 1) * own]], axis=0)
        percore.append({
            "xin": np.ascontiguousarray(xi),
            "cT": np.ascontiguousarray(c[b].reshape(8, 128).T),
            "mcol": np.full((128, 1), 0.0 if q == 0 else 1.0, np.float32),
            "Pfirst": Pfirst if q == 0 else Pcur,
        })
    out = np.zeros((B, SEQ, D), np.float32)
    if fused:
        nc = _get_prog(L, list(range(L)), ntiles, False)
        sh = dict(shared)
        for k in layer_keys:
            sh[k] = np.ascontiguousarray(shared[k][:L])
        in_maps = [dict(sh, **pc) for pc in percore]
        res = run_bass_kernel_spmd(nc, in_maps, core_ids=list(range(n_cores)))
        for i in range(n_cores):
            b, q = divmod(i, cores_per_seq)
            out[b, q * own:(q + 1) * own] = res.results[i]["yout"]
    else:
        nc = _get_prog(1, [0], ntiles, True)
        cur = [pc["xin"] for pc in percore]
        for l in range(L):
            sh = dict(shared)
            for k in layer_keys:
                sh[k] = np.ascontiguousarray(shared[k][l:l + 1])
            in_maps = [dict(sh, **dict(pc, xin=cur[i])) for i, pc in enumerate(percore)]
            res = run_bass_kernel_spmd(nc, in_maps, core_ids=list(range(n_cores)))
            cur = [np.ascontiguousarray(res.results[i]["yout"]) for i in range(n_cores)]
        for i in range(n_cores):
            b, q = divmod(i, cores_per_seq)
            out[b, q * own:(q + 1) * own] = cur[i][512:]
    return out


FUSED = True


def kernel(**inputs):
    return run_model(inputs, fused=FUSED)
```
